# Optimizing a Trainium2 kernel written in Bass

```python
import math
import jax, jax.numpy as jnp
from jax import lax
import numpy as np

D_MODEL = 1024
BATCH = 4
SEQ = 4096
DEPTH = 2

CHUNK = 64
MEM_LEN = 256
QBLOCK = 128
EPS = 1e-6

DIFF_HEADS = 4
DIFF_DK = 64
DIFF_DV = 2 * DIFF_DK
DIFF_SCALE = DIFF_DK ** -0.5
MLA_HEADS = 4
MLA_NOPE = 128
MLA_ROPE = 64
MLA_DV = 128
MLA_Q_LORA = 256
MLA_KV_LORA = 128
MLA_SCALE = (MLA_NOPE + MLA_ROPE) ** -0.5
ROPE_BASE = 10000.0
EVEN_SPLITS = (DIFF_HEADS * 2 * DIFF_DK, DIFF_HEADS * 2 * DIFF_DK, DIFF_HEADS * DIFF_DV, MLA_Q_LORA, MLA_KV_LORA, MLA_ROPE)
EVEN_IN = sum(EVEN_SPLITS)
EVEN_MIX = DIFF_HEADS * DIFF_DV + MLA_HEADS * MLA_DV
SB_HEADS = 16
SB_DH = 64
SB_WIDTH = SB_HEADS * SB_DH
SB_SCALE = SB_DH ** -0.5
XA_HEADS = 4
XA_DH = D_MODEL // XA_HEADS
D_FF = 4 * D_MODEL

N_EVEN = (DEPTH + 1) // 2
N_ODD = DEPTH // 2

kernel_name = 'chunk_causal_hybrid_diff_mla_stickbreak'


def _rms(x, g):
    xf = x.astype(jnp.float32)
    y = xf * lax.rsqrt(jnp.mean(xf * xf, axis=-1, keepdims=True) + EPS)
    return (y * g.astype(jnp.float32)).astype(x.dtype)


def _heads(x, h):
    b, s, _ = x.shape
    return x.reshape(b, s, h, -1).transpose(0, 2, 1, 3)


def _merge(x):
    b, h, s, d = x.shape
    return x.transpose(0, 2, 1, 3).reshape(b, s, h * d)


def _rope(x, pos):
    half = x.shape[-1] // 2
    inv = ROPE_BASE ** (-jnp.arange(half, dtype=jnp.float32) / half)
    ang = pos.astype(jnp.float32)[:, None] * inv[None, :]
    cos, sin = jnp.cos(ang), jnp.sin(ang)
    xf = x.astype(jnp.float32)
    x1, x2 = xf[..., :half], xf[..., half:]
    return jnp.concatenate([x1 * cos - x2 * sin, x2 * cos + x1 * sin], axis=-1).astype(x.dtype)


def _chunk_mask(t, s):
    return (s // CHUNK)[None, :] <= (t // CHUNK)[:, None]


def _sweep(block_fn, qs):
    b, _, s, _ = qs[0].shape
    nb = s // QBLOCK
    qs_b = tuple(jnp.moveaxis(q.reshape(b, q.shape[1], nb, QBLOCK, q.shape[-1]), 2, 0) for q in qs)
    out = lax.map(lambda a: block_fn(a[0], *a[1]), (jnp.arange(nb), qs_b))
    out = jnp.moveaxis(out, 0, 2)
    return out.reshape(b, out.shape[1], s, out.shape[-1])


def _even_mixer(h, w_in, lq1, lk1, lq2, lk2, g_sub, g_cq, w_uq, g_ckv, w_ukv, w_out, lambda_init):
    b, s, _ = h.shape
    pos = jnp.arange(s)
    proj = h @ w_in
    cuts = np.cumsum(EVEN_SPLITS)[:-1].tolist()
    aq, ak, av, cq, ckv, kr = jnp.split(proj, cuts, axis=-1)

    aq = aq.reshape(b, s, DIFF_HEADS, 2, DIFF_DK)
    ak = ak.reshape(b, s, DIFF_HEADS, 2, DIFF_DK)
    q1 = aq[:, :, :, 0].transpose(0, 2, 1, 3)
    q2 = aq[:, :, :, 1].transpose(0, 2, 1, 3)
    k1 = ak[:, :, :, 0].transpose(0, 2, 1, 3)
    k2 = ak[:, :, :, 1].transpose(0, 2, 1, 3)
    va = _heads(av, DIFF_HEADS)
    lam = (jnp.exp(jnp.sum(lq1.astype(jnp.float32) * lk1.astype(jnp.float32)))
           - jnp.exp(jnp.sum(lq2.astype(jnp.float32) * lk2.astype(jnp.float32))) + lambda_init)
    slopes = 2.0 ** (-8.0 * jnp.arange(1, DIFF_HEADS + 1, dtype=jnp.float32) / DIFF_HEADS)

    def diff_block(i, q1b, q2b):
        t = i * QBLOCK + jnp.arange(QBLOCK)
        mask = _chunk_mask(t, pos)
        dist = jnp.abs(t[:, None] - pos[None, :]).astype(jnp.float32)
        bias = -slopes[:, None, None] * dist[None]

        def probs(qb, kk):
            sc = jnp.einsum('bhqd,bhkd->bhqk', qb, kk).astype(jnp.float32) * DIFF_SCALE + bias
            return jax.nn.softmax(jnp.where(mask, sc, -jnp.inf), axis=-1)

        p = probs(q1b, k1) - lam * probs(q2b, k2)
        return jnp.einsum('bhqk,bhkd->bhqd', p.astype(va.dtype), va)

    o_a = _sweep(diff_block, (q1, q2))
    o_a = _rms(o_a, g_sub) * (1.0 - lambda_init)

    q = _heads(_rms(cq, g_cq) @ w_uq, MLA_HEADS)
    q_nope, q_rope = q[..., :MLA_NOPE], _rope(q[..., MLA_NOPE:], pos)
    kv = _heads(_rms(ckv, g_ckv) @ w_ukv, MLA_HEADS)
    k_nope, vb = kv[..., :MLA_NOPE], kv[..., MLA_NOPE:]
    k_rope = _rope(kr, pos)

    def mla_block(i, qnb, qrb):
        t = i * QBLOCK + jnp.arange(QBLOCK)
        mask = _chunk_mask(t, pos)
        sc = (jnp.einsum('bhqd,bhkd->bhqk', qnb, k_nope)
              + jnp.einsum('bhqr,bkr->bhqk', qrb, k_rope)).astype(jnp.float32) * MLA_SCALE
        p = jax.nn.softmax(jnp.where(mask, sc, -jnp.inf), axis=-1)
        return jnp.einsum('bhqk,bhkd->bhqd', p.astype(vb.dtype), vb)

    o_b = _sweep(mla_block, (q_nope, q_rope))

    return jnp.concatenate([_merge(o_a), _merge(o_b)], axis=-1) @ w_out


def _odd_mixer(h, w_in, w_out):
    s = h.shape[1]
    pos = jnp.arange(s)
    q, k, v = jnp.split(h @ w_in, 3, axis=-1)
    q, k, v = _heads(q, SB_HEADS), _heads(k, SB_HEADS), _heads(v, SB_HEADS)

    def sb_block(i, qb):
        t = i * QBLOCK + jnp.arange(QBLOCK)
        strict = pos[None, :] < t[:, None]
        z = jnp.einsum('bhqd,bhkd->bhqk', qb, k).astype(jnp.float32) * SB_SCALE
        log_beta = jax.nn.log_sigmoid(z)
        log_keep = jnp.where(strict, jax.nn.log_sigmoid(-z), 0.0)
        between = lax.cumsum(log_keep, axis=3, reverse=True) - log_keep
        a = jnp.where(strict, jnp.exp(log_beta + between), 0.0)
        return jnp.einsum('bhqk,bhkd->bhqd', a.astype(v.dtype), v)

    o = _sweep(sb_block, (q,))
    return _merge(o) @ w_out


def _cross(h, mem_n, wq, wkv, wo):
    q = _heads(h @ wq, XA_HEADS)
    k, v = jnp.split(mem_n @ wkv, 2, axis=-1)
    k, v = _heads(k, XA_HEADS), _heads(v, XA_HEADS)
    sc = jnp.einsum('bhqd,bhkd->bhqk', q, k).astype(jnp.float32) * (XA_DH ** -0.5)
    p = jax.nn.softmax(sc, axis=-1)
    return _merge(jnp.einsum('bhqk,bhkd->bhqd', p.astype(v.dtype), v)) @ wo


def _mlp(h, w1, w2):
    return jnp.square(jax.nn.relu(h @ w1)) @ w2


def setup_inputs(seed: int = 0) -> dict:
    key = jax.random.key(seed)
    ks = iter(jax.random.split(key, 40))

    def nrm(shape, scale):
        return jax.random.normal(next(ks), shape, jnp.float32) * scale

    def gain(shape):
        return 1.0 + nrm(shape, 0.02)

    L, E, O, D = DEPTH, N_EVEN, N_ODD, D_MODEL
    return {
        'x': nrm((BATCH, SEQ, D), 1.0),
        'mem': nrm((BATCH, MEM_LEN, D), 1.0),
        'ev_norm': gain((E, D)),
        'ev_w_in': nrm((E, D, EVEN_IN), D ** -0.5),
        'diff_lq1': nrm((E, DIFF_DK), 0.1),
        'diff_lk1': nrm((E, DIFF_DK), 0.1),
        'diff_lq2': nrm((E, DIFF_DK), 0.1),
        'diff_lk2': nrm((E, DIFF_DK), 0.1),
        'diff_subln': gain((E, DIFF_DV)),
        'mla_g_cq': gain((E, MLA_Q_LORA)),
        'mla_w_uq': nrm((E, MLA_Q_LORA, MLA_HEADS * (MLA_NOPE + MLA_ROPE)), MLA_Q_LORA ** -0.5),
        'mla_g_ckv': gain((E, MLA_KV_LORA)),
        'mla_w_ukv': nrm((E, MLA_KV_LORA, MLA_HEADS * (MLA_NOPE + MLA_DV)), MLA_KV_LORA ** -0.5),
        'ev_w_out': nrm((E, EVEN_MIX, D), EVEN_MIX ** -0.5),
        'od_norm': gain((O, D)),
        'sb_w_in': nrm((O, D, 3 * SB_WIDTH), D ** -0.5),
        'sb_w_out': nrm((O, SB_WIDTH, D), SB_WIDTH ** -0.5),
        'xa_norm': gain((L, D)),
        'xa_mem_norm': gain((L, D)),
        'xa_wq': nrm((L, D, D), D ** -0.5),
        'xa_wkv': nrm((L, D, 2 * D), D ** -0.5),
        'xa_wo': nrm((L, D, D), D ** -0.5),
        'mlp_norm': gain((L, D)),
        'mlp_w1': nrm((L, D, D_FF), D ** -0.5),
        'mlp_w2': nrm((L, D_FF, D), D_FF ** -0.5),
        'final_norm': gain((D,)),
    }


def reference(x, mem, ev_norm, ev_w_in, diff_lq1, diff_lk1, diff_lq2, diff_lk2, diff_subln,
              mla_g_cq, mla_w_uq, mla_g_ckv, mla_w_ukv, ev_w_out, od_norm, sb_w_in, sb_w_out,
              xa_norm, xa_mem_norm, xa_wq, xa_wkv, xa_wo, mlp_norm, mlp_w1, mlp_w2, final_norm):
    h = x
    for i in range(DEPTH):
        j = i // 2
        if i % 2 == 0:
            lambda_init = 0.8 - 0.6 * math.exp(-0.3 * i)
            h = h + _even_mixer(_rms(h, ev_norm[j]), ev_w_in[j], diff_lq1[j], diff_lk1[j], diff_lq2[j],
                                diff_lk2[j], diff_subln[j], mla_g_cq[j], mla_w_uq[j], mla_g_ckv[j],
                                mla_w_ukv[j], ev_w_out[j], lambda_init)
        else:
            h = h + _odd_mixer(_rms(h, od_norm[j]), sb_w_in[j], sb_w_out[j])
        h = h + _cross(_rms(h, xa_norm[i]), _rms(mem, xa_mem_norm[i]), xa_wq[i], xa_wkv[i], xa_wo[i])
        h = h + _mlp(_rms(h, mlp_norm[i]), mlp_w1[i], mlp_w2[i])
    return _rms(h, final_norm)
```

```python
import math
from contextlib import ExitStack

import numpy as np
import concourse.bass as bass
import concourse.mybir as mybir
from concourse.bass_utils import run_bass_kernel_spmd

F32 = mybir.dt.float32
BF16 = mybir.dt.bfloat16
AF = mybir.ActivationFunctionType
ALU = mybir.AluOpType

S = 4096
D = 1024
OWN = 2048
NB = 32
EPS = 1e-6
DIFF_SCALE = 64 ** -0.5
MLA_SCALE = 192 ** -0.5
XA_SCALE = 256 ** -0.5
LAMBDA_INIT0 = 0.8 - 0.6 * math.exp(-0.3 * 0)
NEG = -30000.0

ENGS = ("pe", "act", "dve", "pool", "sp")
DMAQ = ("sp", "pool")


class Res:
    __slots__ = ("name", "w", "rs")

    def __init__(self, name=""):
        self.name = name
        self.w = None
        self.rs = []


class Op:
    __slots__ = ("eng", "fn", "need", "signal", "cnt", "dma", "sem", "semval", "phase", "inc")

    def __init__(self, eng, fn, dma, phase):
        self.eng = eng
        self.fn = fn
        self.need = []
        self.signal = False
        self.cnt = 0
        self.dma = dma
        self.sem = None
        self.semval = 0
        self.phase = phase
        self.inc = 16


class Prog:
    NRING = {"sp": 12, "pool": 6}

    def __init__(self, nc, stack):
        self.nc = nc
        self.phase = 0
        self.ops = {e: [] for e in ENGS}
        self.csem = {}
        self.ccnt = {}
        for e in ("pe", "act", "dve", "pool"):
            self.csem[e] = stack.enter_context(nc.semaphore("c_" + e))
            self.ccnt[e] = 0
        self.ring = {}
        self.ringval = {}
        self.dk = {}
        for q in DMAQ:
            self.ring[q] = [stack.enter_context(nc.semaphore("d_%s%d" % (q, i))) for i in range(self.NRING[q])]
            self.ringval[q] = [0] * self.NRING[q]
            self.dk[q] = 0
        self.ccsem = stack.enter_context(nc.semaphore("ccsem"))
        self.ccval = 0
        self.waited = {e: {} for e in ENGS}
        self.nops = 0

    def add(self, eng, fn, reads=(), writes=(), dma=False, cc=False):
        op = Op(eng, fn, dma or cc, self.phase)
        raw = set()
        other = set()
        for r in reads:
            if r.w is not None:
                raw.add(r.w)
        for r in writes:
            if r.w is not None:
                other.add(r.w)
            for o in r.rs:
                other.add(o)
        for d in raw | other:
            if d.phase != self.phase or d is op:
                continue
            if not d.dma and d.eng == eng:
                if eng == "pe":
                    continue
                if d not in raw:
                    continue
            op.need.append(d)
        if cc:
            self.ccval += 1
            op.sem = self.ccsem
            op.semval = self.ccval
            op.inc = 1
        elif dma:
            q = eng
            i = self.dk[q] % self.NRING[q]
            self.dk[q] += 1
            self.ringval[q][i] += 16
            op.sem = self.ring[q][i]
            op.semval = self.ringval[q][i]
        for r in reads:
            r.rs.append(op)
        for r in writes:
            r.w = op
            r.rs = []
        self.ops[eng].append(op)
        self.nops += 1
        return op

    def flush(self):
        nc = self.nc
        lasts = {}
        for e in ("pe", "act", "dve", "pool"):
            for op in reversed(self.ops[e]):
                if not op.dma:
                    lasts[e] = op
                    break
        for e in ENGS:
            for op in self.ops[e]:
                for d in op.need:
                    d.signal = True
        for op in lasts.values():
            op.signal = True
        for e in ("pe", "act", "dve", "pool"):
            c = self.ccnt[e]
            for op in self.ops[e]:
                if op.signal and not op.dma:
                    c += 1
                    op.cnt = c
            self.ccnt[e] = c
        ringfinal = {q: list(self.ringval[q]) for q in DMAQ}
        ccfinal = self.ccval

        def emit(ename, eng):
            w = self.waited[ename]

            def wait(sem, key, val):
                if w.get(key, 0) < val:
                    eng.wait_ge(sem, val)
                    w[key] = val

            for op in self.ops[ename]:
                for d in op.need:
                    if d.dma:
                        wait(d.sem, id(d.sem), d.semval)
                    else:
                        wait(self.csem[d.eng], d.eng, d.cnt)
                if op.dma and op.inc == 16 and op.semval > 16:
                    wait(op.sem, id(op.sem), op.semval - 16)
                ins = op.fn(eng)
                if op.dma:
                    ins.then_inc(op.sem, op.inc)
                elif op.signal:
                    ins.then_inc(self.csem[ename], 1)
            for e2 in ("pe", "act", "dve", "pool"):
                if self.ccnt[e2] > 0:
                    wait(self.csem[e2], e2, self.ccnt[e2])
            for q in DMAQ:
                for i, s in enumerate(self.ring[q]):
                    if ringfinal[q][i] > 0:
                        wait(s, id(s), ringfinal[q][i])
            if ccfinal > 0:
                wait(self.ccsem, id(self.ccsem), ccfinal)

        with nc.Block() as block:
            @block.tensor
            def _(eng):
                emit("pe", eng)

            @block.scalar
            def _(eng):
                emit("act", eng)

            @block.vector
            def _(eng):
                emit("dve", eng)

            @block.gpsimd
            def _(eng):
                emit("pool", eng)

            @block.sync
            def _(eng):
                emit("sp", eng)

        self.ops = {e: [] for e in ENGS}
        self.phase += 1


class Buf:
    def __init__(self, t, name=""):
        self.t = t
        self.r = Res(name)


class Ring:
    def __init__(self, bufs):
        self.bufs = bufs
        self.i = 0

    def next(self):
        b = self.bufs[self.i % len(self.bufs)]
        self.i += 1
        return b


class K:
    def __init__(self, nc, st):
        self.nc = nc
        self.P = Prog(nc, st)
        self.flip = 0
        self.uid = 0

    def sb(self, st, name, shape, dt, side=None):
        self.uid += 1
        return Buf(st.enter_context(self.nc.sbuf_tensor("s%d_%s" % (self.uid, name), shape, dt, side=side)), name)

    def ps(self, st, name, shape=(128, 512), dt=F32):
        self.uid += 1
        return Buf(st.enter_context(self.nc.psum_tensor("p%d_%s" % (self.uid, name), list(shape), dt)), name)

    def dma(self, q, out, in_, reads, writes):
        return self.P.add(q, lambda e: e.dma_start(out=out, in_=in_), reads, writes, dma=True)

    def mm(self, out, lhsT, rhs, start, stop, reads, writes):
        return self.P.add("pe", lambda e: e.matmul(out, lhsT=lhsT, rhs=rhs, start=start, stop=stop), reads, writes)

    def act(self, out, in_, func, reads, writes, **kw):
        return self.P.add("act", lambda e: e.activation(out=out, in_=in_, func=func, **kw), reads, writes)

    def tt(self, eng, out, in0, in1, op, reads, writes):
        return self.P.add(eng, lambda e: e.tensor_tensor(out=out, in0=in0, in1=in1, op=op), reads, writes)

    def ts(self, eng, out, in0, s1, s2, op0, op1, reads, writes):
        if s2 is None:
            return self.P.add(eng, lambda e: e.tensor_scalar(out=out, in0=in0, scalar1=s1, scalar2=None, op0=op0), reads, writes)
        return self.P.add(eng, lambda e: e.tensor_scalar(out=out, in0=in0, scalar1=s1, scalar2=s2, op0=op0, op1=op1), reads, writes)

    def stt(self, eng, out, in0, scalar, in1, op0, op1, reads, writes):
        return self.P.add(eng, lambda e: e.scalar_tensor_tensor(out=out, in0=in0, scalar=scalar, in1=in1, op0=op0, op1=op1), reads, writes)

    def copy(self, eng, out, in_, reads, writes):
        if eng == "act":
            return self.P.add("act", lambda e: e.copy(out=out, in_=in_), reads, writes)
        return self.P.add(eng, lambda e: e.tensor_copy(out=out, in_=in_), reads, writes)

    def evac(self, out, in_, reads, writes):
        self.flip ^= 1
        return self.copy("act" if self.flip else "dve", out, in_, reads, writes)

    def memset(self, eng, ap, val, writes):
        return self.P.add(eng, lambda e: e.memset(ap, val), [], writes)

    def recip(self, out, in_, reads, writes):
        return self.P.add("dve", lambda e: e.reciprocal(out=out, in_=in_), reads, writes)

    def rstd_chain(self, ss, rstd, n, inv_n, post=None):
        self.ts("dve", rstd.t[:, 0:n], ss.t[:, 0:n], inv_n, EPS, ALU.mult, ALU.add, [ss.r], [rstd.r])
        self.act(rstd.t[:, 0:n], rstd.t[:, 0:n], AF.Sqrt, [rstd.r], [rstd.r])
        self.recip(rstd.t[:, 0:n], rstd.t[:, 0:n], [rstd.r], [rstd.r])
        if post is not None:
            self.ts("dve", rstd.t[:, 0:n], rstd.t[:, 0:n], post, None, ALU.mult, None, [rstd.r], [rstd.r])

    def rmsnorm(self, src_ap, src_r, nblk, g, dst_ap, dst_r, junk, ss, rstd):
        srs = (lambda j: [src_r]) if isinstance(src_r, Res) else src_r
        self.memset("dve", ss.t[:, 0:nblk], 0.0, [ss.r])
        for j in range(nblk):
            self.act(junk.t[:, :], src_ap(j), AF.Square, srs(j) + [ss.r], [junk.r, ss.r], accum_out=ss.t[:, j:j + 1])
        self.rstd_chain(ss, rstd, nblk, 1.0 / D)
        for j in range(nblk):
            self.stt("dve", dst_ap(j), src_ap(j), rstd.t[:, j:j + 1], g.t[:, :], ALU.mult, ALU.mult,
                     srs(j) + [rstd.r, g.r], [dst_r])

    def transpose(self, src_ap, src_rs, nblk, nchunk, ident, dst_ap, dst_r, psring):
        for k in range(nchunk):
            ps = psring.next()
            for j in range(nblk):
                self.mm(ps.t[:, j * 128:(j + 1) * 128], src_ap(j, k), ident.t[:, :], True, True,
                        list(src_rs) + [ident.r], [ps.r])
            self.evac(dst_ap(k), ps.t[:, 0:nblk * 128], [ps.r], [dst_r])


def build_program(stop_after=None):
    nc = bass.Bass("TRN2", target_bir_lowering=False)

    def din(name, shape, dt=F32):
        return nc.dram_tensor(name, list(shape), dt, kind="ExternalInput").ap()

    def dscr(name, shape, dt=BF16):
        return nc.dram_tensor(name, list(shape), dt).ap()

    I = {}
    I["x_full"] = din("x_full", [S, D])
    I["x_own"] = din("x_own", [OWN, D])
    I["mem"] = din("mem", [256, D])
    I["gains"] = din("gains", [9, 128, D])
    I["w_in0"] = din("w_in0", [D, 1280])
    I["w_uq"] = din("w_uq", [256, 512])
    I["w_ukv"] = din("w_ukv", [128, 512])
    I["w_out0"] = din("w_out0", [D, D])
    I["g_cq"] = din("g_cq", [128, 2])
    I["g_ckv"] = din("g_ckv", [128, 1])
    I["g_sub"] = din("g_sub", [128, 128])
    I["lamv"] = din("lamv", [128, 4, 64])
    I["w_sb"] = din("w_sb", [D, 1536])
    I["w_sbo"] = din("w_sbo", [D, D])
    I["xa_wq"] = din("xa_wq", [2, D, D])
    I["xa_wkv"] = din("xa_wkv", [2, D, 2 * D])
    I["xa_wo"] = din("xa_wo", [2, D, D])
    I["w1"] = din("w1", [2, D, 4 * D])
    I["w2"] = din("w2", [2, 4 * D, D])
    I["cmat"] = din("cmat", [7, 128, 128])
    I["rope"] = din("rope", [2, 64, S])
    I["btab"] = din("btab", [2, 128, 32])
    I["dtab"] = din("dtab", [7, 128, 128])
    y_out = nc.dram_tensor("y", [OWN, D], F32, kind="ExternalOutput").ap()

    QaT = dscr("QaT", [2, 128, S]); KaT = dscr("KaT", [2, 128, S]); Va = dscr("Va", [2, S, 128])
    QnT = dscr("QnT", [2, 128, S]); QrT = dscr("QrT", [2, 64, S]); KnT = dscr("KnT", [2, 128, S])
    KrT = dscr("KrT", [64, S]); Vb = dscr("Vb", [2, S, 128])
    Osc = [dscr("O0", [S, 512]), dscr("O1", [S, 512])]
    Og = [dscr("O0g", [2, 2 * OWN, 512]), dscr("O1g", [2, 2 * OWN, 512])]
    Hn = dscr("Hn", [OWN, D]); Hg = dscr("Hg", [2, 2 * 1024, D])
    QsT = dscr("QsT", [4, 128, S]); KsT = dscr("KsT", [4, 128, S]); Vs = dscr("Vs", [S, 512])
    dres = Res("dram")

    RG = [[0, 1], [2, 3], [4, 5], [6, 7]]

    with ExitStack() as st:
        k = K(nc, st)
        P = k.P
        h = k.sb(st, "h", [128, 16, D], F32)
        hr = [[Res("h%d_%d" % (j, hf)) for hf in range(2)] for j in range(16)]
        hall = [r_ for row in hr for r_ in row]
        cm = k.sb(st, "cm", [128, 7, 128], BF16)
        ones32 = k.sb(st, "ones32", [128, 128], F32)
        lam = k.sb(st, "lam", [128, 4], F32)
        junk = k.sb(st, "junk", [128, D], F32)
        ss = k.sb(st, "ss", [128, 16], F32)
        rstd = k.sb(st, "rstd", [128, 16], F32)
        ident = Buf(cm.t, "ident")
        ident.r = cm.r

        def IDN():
            return cm.t[:, 0, :]

        class _Id:
            pass

        identb = _Id()
        identb.t = cm.t[:, 0, :]
        identb.r = cm.r

        with ExitStack() as s0:
            lv = k.sb(s0, "lv", [128, 4, 64], F32)
            lt = k.sb(s0, "lt", [128, 2], F32)
            k.dma("pool", cm.t[:, :, :], I["cmat"].rearrange("n p f -> p n f"), [], [cm.r])
            k.memset("dve", ones32.t[:, :], 1.0, [ones32.r])
            k.dma("sp", lv.t[:, :, :], I["lamv"][:, :, :], [], [lv.r])
            k.dma("sp", h.t[:, :, :], I["x_own"].rearrange("(n p) f -> p n f", p=128), [], hall)
            k.memset("dve", lt.t[:, :], 0.0, [lt.r])
            for i in range(2):
                k.P.add("dve", lambda e, i=i: e.tensor_tensor(out=junk.t[:, 0:64], in0=lv.t[:, 2 * i, :], in1=lv.t[:, 2 * i + 1, :], op=ALU.mult),
                        [lv.r], [junk.r])
                k.P.add("dve", lambda e, i=i: e.reduce_sum(out=lt.t[:, i:i + 1], in_=junk.t[:, 0:64], axis=mybir.AxisListType.X),
                        [junk.r], [lt.r])
            k.act(lt.t[:, :], lt.t[:, :], AF.Exp, [lt.r], [lt.r])
            k.stt("dve", lam.t[:, 0:1], lt.t[:, 1:2], -LAMBDA_INIT0, lt.t[:, 0:1], ALU.add, ALU.subtract, [lt.r], [lam.r])
            P.flush()

        def pipeline(items, stages):
            n = len(items)
            ks = len(stages)
            for t in range(n + ks - 1):
                for j, f in enumerate(stages):
                    i = t - j
                    if 0 <= i < n:
                        f(items[i])

        def phase_P0():
            with ExitStack() as s:
                win = k.sb(s, "win", [128, 8, 1280], BF16)
                wuq = k.sb(s, "wuq", [128, 2, 512], BF16)
                wukv = k.sb(s, "wukv", [128, 512], BF16)
                gev = k.sb(s, "gev", [128, D], F32)
                gcq = k.sb(s, "gcq", [128, 2], F32)
                gckv = k.sb(s, "gckv", [128, 1], F32)
                xt = k.sb(s, "xt", [128, 4, D], F32)
                xn = k.sb(s, "xn", [128, 4, D], BF16)
                xnT = [k.sb(s, "xnT%d" % i, [128, 8, 512], BF16) for i in range(2)]
                rope = [k.sb(s, "rope%d" % i, [64, 2, 512], F32) for i in range(3)]
                osb = Ring([k.sb(s, "osb%d" % i, [128, 512], BF16) for i in range(6)])
                vsb = Ring([k.sb(s, "vsb%d" % i, [128, 4, 256], BF16) for i in range(3)])
                sq = [k.sb(s, "sq%d" % i, [128, 512], F32) for i in range(3)]
                rsb = k.sb(s, "rsb", [128, 512], F32)
                rsb2 = k.sb(s, "rsb2", [128, 512], F32)
                cqn = [k.sb(s, "cqn%d" % i, [128, 2, 512], BF16) for i in range(2)]
                ckvn = [k.sb(s, "ckvn%d" % i, [128, 512], BF16) for i in range(2)]
                rr = [k.sb(s, "rr%d" % i, [64, 512], F32) for i in range(4)]
                ss0 = k.sb(s, "ss0", [128, 4], F32)
                rstd0 = k.sb(s, "rstd0", [128, 4], F32)
                psr = Ring([k.ps(s, "ps%d" % i) for i in range(8)])
                win_r = [Res("winA"), Res("winB"), Res("winC")]
                for gi_, (c0_, c1_) in enumerate([(0, 512), (512, 768), (768, 1280)]):
                    k.dma("pool", win.t[:, :, c0_:c1_], I["w_in0"][:, c0_:c1_].rearrange("(k p) c -> p k c", p=128), [], [win_r[gi_]])
                k.dma("pool", wuq.t[:, :, :], I["w_uq"].rearrange("(k p) c -> p k c", p=128), [], [wuq.r])
                k.dma("pool", wukv.t[:, :], I["w_ukv"][:, :], [], [wukv.r])
                k.dma("sp", gev.t[:, :], I["gains"][0], [], [gev.r])
                k.dma("sp", gcq.t[:, :], I["g_cq"][:, :], [], [gcq.r])
                k.dma("sp", gckv.t[:, :], I["g_ckv"][:, :], [], [gckv.r])

                def rope_apply(psA, psB, rp, out_ap, out_r, ra, rb):
                    k.tt("dve", ra.t[:, :], psA.t[0:64, :], rp.t[:, 0, :], ALU.mult, [psA.r, rp.r], [ra.r])
                    k.tt("dve", rb.t[:, :], psB.t[0:64, :], rp.t[:, 1, :], ALU.mult, [psB.r, rp.r], [rb.r])
                    k.tt("dve", out_ap, ra.t[:, :], rb.t[:, :], ALU.add, [ra.r, rb.r], [out_r])

                def sA(t):
                    tok = slice(t * 512, (t + 1) * 512)
                    rp = rope[t % 3]
                    X = xnT[t % 2]
                    k.dma("sp", xt.t[:, :, :], I["x_full"][tok, :].rearrange("(n p) f -> p n f", p=128), [], [xt.r])
                    k.dma("sp", rp.t[:, :, :], I["rope"][:, :, tok].rearrange("n p f -> p n f"), [], [rp.r])
                    k.rmsnorm(lambda j: xt.t[:, j, :], xt.r, 4, gev, lambda j: xn.t[:, j, :], xn.r, junk, ss0, rstd0)
                    k.transpose(lambda j, c: xn.t[:, j, c * 128:(c + 1) * 128], [xn.r], 4, 8, identb, lambda c: X.t[:, c, :], X.r, psr)

                def sB(t):
                    tok = slice(t * 512, (t + 1) * 512)
                    rp = rope[t % 3]
                    X = xnT[t % 2]
                    cq_, ckv_ = cqn[t % 2], ckvn[t % 2]
                    for ci, (dst, hh) in enumerate([(QaT, 0), (QaT, 1), (KaT, 0), (KaT, 1)]):
                        ps = psr.next()
                        for kk in range(8):
                            k.mm(ps.t[:, :], win.t[:, kk, ci * 128:(ci + 1) * 128], X.t[:, kk, :], kk == 0, kk == 7, [win_r[0], X.r], [ps.r])
                        o = osb.next()
                        k.evac(o.t[:, :], ps.t[:, :], [ps.r], [o.r])
                        k.dma("sp", dst[hh][:, tok], o.t[:, :], [o.r], [dres])
                    vo = vsb.next()
                    for j in range(4):
                        ps = psr.next()
                        for kk in range(8):
                            k.mm(ps.t[:, 0:256], X.t[:, kk, j * 128:(j + 1) * 128], win.t[:, kk, 512:768], kk == 0, kk == 7, [win_r[1], X.r], [ps.r])
                        k.evac(vo.t[:, j, :], ps.t[:, 0:256], [ps.r], [vo.r])
                    for hh in range(2):
                        k.dma("sp", Va[hh][tok, :].rearrange("(n p) f -> p n f", p=128), vo.t[:, :, hh * 128:(hh + 1) * 128], [vo.r], [dres])
                    lat = []
                    for ci in range(3):
                        ps = psr.next()
                        c0 = 768 + ci * 128
                        for kk in range(8):
                            k.mm(ps.t[:, :], win.t[:, kk, c0:c0 + 128], X.t[:, kk, :], kk == 0, kk == 7, [win_r[2], X.r], [ps.r])
                        k.act(sq[ci].t[:, :], ps.t[:, :], AF.Square, [ps.r], [sq[ci].r])
                        lat.append(ps)
                    pss = psr.next()
                    for ci in range(2):
                        k.mm(pss.t[:, :], ones32.t[:, :], sq[ci].t[:, :], ci == 0, ci == 1, [ones32.r, sq[ci].r], [pss.r])
                    pss2 = psr.next()
                    k.mm(pss2.t[:, :], ones32.t[:, :], sq[2].t[:, :], True, True, [ones32.r, sq[2].r], [pss2.r])
                    psA = psr.next()
                    psB = psr.next()
                    for kk in range(8):
                        k.mm(psA.t[0:64, :], win.t[:, kk, 1152:1216], X.t[:, kk, :], kk == 0, kk == 7, [win_r[2], X.r], [psA.r])
                    for kk in range(8):
                        k.mm(psB.t[0:64, :], win.t[:, kk, 1216:1280], X.t[:, kk, :], kk == 0, kk == 7, [win_r[2], X.r], [psB.r])
                    k.ts("dve", rsb.t[:, :], pss.t[:, :], 1.0 / 256, EPS, ALU.mult, ALU.add, [pss.r], [rsb.r])
                    k.ts("dve", rsb2.t[:, :], pss2.t[:, :], 1.0 / 128, EPS, ALU.mult, ALU.add, [pss2.r], [rsb2.r])
                    k.act(rsb.t[:, :], rsb.t[:, :], AF.Sqrt, [rsb.r], [rsb.r])
                    k.act(rsb2.t[:, :], rsb2.t[:, :], AF.Sqrt, [rsb2.r], [rsb2.r])
                    k.recip(rsb.t[:, :], rsb.t[:, :], [rsb.r], [rsb.r])
                    k.recip(rsb2.t[:, :], rsb2.t[:, :], [rsb2.r], [rsb2.r])
                    for ci in range(2):
                        k.stt("dve", cq_.t[:, ci, :], lat[ci].t[:, :], gcq.t[:, ci:ci + 1], rsb.t[:, :], ALU.mult, ALU.mult,
                              [lat[ci].r, gcq.r, rsb.r], [cq_.r])
                    k.stt("dve", ckv_.t[:, :], lat[2].t[:, :], gckv.t[:, 0:1], rsb2.t[:, :], ALU.mult, ALU.mult,
                          [lat[2].r, gckv.r, rsb2.r], [ckv_.r])
                    o = osb.next()
                    rope_apply(psA, psB, rp, o.t[0:64, :], o.r, rr[0], rr[1])
                    k.dma("sp", KrT[:, tok], o.t[0:64, :], [o.r], [dres])

                def sC(t):
                    tok = slice(t * 512, (t + 1) * 512)
                    rp = rope[t % 3]
                    cq_, ckv_ = cqn[t % 2], ckvn[t % 2]
                    for hh in range(2):
                        ps = psr.next()
                        for ci in range(2):
                            k.mm(ps.t[:, :], wuq.t[:, ci, hh * 128:(hh + 1) * 128], cq_.t[:, ci, :], ci == 0, ci == 1, [wuq.r, cq_.r], [ps.r])
                        o = osb.next()
                        k.evac(o.t[:, :], ps.t[:, :], [ps.r], [o.r])
                        k.dma("sp", QnT[hh][:, tok], o.t[:, :], [o.r], [dres])
                        psA = psr.next()
                        psB = psr.next()
                        c0 = 256 + hh * 128
                        for ci in range(2):
                            k.mm(psA.t[0:64, :], wuq.t[:, ci, c0:c0 + 64], cq_.t[:, ci, :], ci == 0, ci == 1, [wuq.r, cq_.r], [psA.r])
                        for ci in range(2):
                            k.mm(psB.t[0:64, :], wuq.t[:, ci, c0 + 64:c0 + 128], cq_.t[:, ci, :], ci == 0, ci == 1, [wuq.r, cq_.r], [psB.r])
                        o = osb.next()
                        rope_apply(psA, psB, rp, o.t[0:64, :], o.r, rr[2], rr[3])
                        k.dma("sp", QrT[hh][:, tok], o.t[0:64, :], [o.r], [dres])
                    for hh in range(2):
                        ps = psr.next()
                        k.mm(ps.t[:, :], wukv.t[:, hh * 128:(hh + 1) * 128], ckv_.t[:, :], True, True, [wukv.r, ckv_.r], [ps.r])
                        o = osb.next()
                        k.evac(o.t[:, :], ps.t[:, :], [ps.r], [o.r])
                        k.dma("sp", KnT[hh][:, tok], o.t[:, :], [o.r], [dres])
                    vo = vsb.next()
                    for j in range(4):
                        ps = psr.next()
                        k.mm(ps.t[:, 0:256], ckv_.t[:, j * 128:(j + 1) * 128], wukv.t[:, 256:512], True, True, [wukv.r, ckv_.r], [ps.r])
                        k.evac(vo.t[:, j, :], ps.t[:, 0:256], [ps.r], [vo.r])
                    for hh in range(2):
                        k.dma("sp", Vb[hh][tok, :].rearrange("(n p) f -> p n f", p=128), vo.t[:, :, hh * 128:(hh + 1) * 128], [vo.r], [dres])

                pipeline(list(range(8)), [sA, sB, sC])
                P.flush()

        def phase_A0(pre=None):
            with ExitStack() as s:
                if pre is not None:
                    pre()
                kt = [k.sb(s, "kt%d" % i, [128, S], BF16) for i in range(2)]
                qt = [k.sb(s, "qt%d" % i, [128, S], BF16) for i in range(2)]
                kr = k.sb(s, "kr", [64, S], BF16)
                qr = [k.sb(s, "qr%d" % i, [64, S], BF16) for i in range(2)]
                vt = [k.sb(s, "vt%d" % i, [128, NB, 129], BF16) for i in range(2)]
                btab = k.sb(s, "btab", [128, 2, 32], F32)
                dtab = k.sb(s, "dtab", [128, 7, 128], F32)
                gsub = k.sb(s, "gsub", [128, 128], F32)
                tmpd = Ring([k.sb(s, "tmpd%d" % i, [128, 128], F32) for i in range(2)])
                pring = Ring([k.sb(s, "pT%d" % i, [128, 512], BF16) for i in range(3)])
                o1n = k.sb(s, "o1n", [128, 4, 128], F32)
                o2n = k.sb(s, "o2n", [128, 4, 128], F32)
                od = k.sb(s, "od", [128, 4, 128], F32)
                obf = Ring([k.sb(s, "obf%d" % i, [128, 4, 128], BF16) for i in range(2)])
                rs = k.sb(s, "rs", [128, 4], F32)
                ss2 = k.sb(s, "ss2", [128, 4], F32)
                rstd2 = k.sb(s, "rstd2", [128, 4], F32)
                sring = Ring([k.ps(s, "sps%d" % i) for i in range(3)])
                oring = Ring([k.ps(s, "ops%d" % i) for i in range(4)])
                k.dma("sp", btab.t[:, :, :], I["btab"].rearrange("n p f -> p n f"), [], [btab.r])
                k.dma("sp", dtab.t[:, :, :], I["dtab"].rearrange("n p f -> p n f"), [], [dtab.r])
                k.dma("sp", gsub.t[:, :], I["g_sub"][:, :], [], [gsub.r])
                k.ts("dve", gsub.t[:, :], gsub.t[:, :], 1.0 - LAMBDA_INIT0, None, ALU.mult, None, [gsub.r], [gsub.r])
                for i in range(2):
                    k.memset("pool", vt[i].t[:, :, :], 1.0, [vt[i].r])
                k.dma("sp", kr.t[:, :], KrT[:, :], [dres], [kr.r])

                def load_diff(hh):
                    k.dma("sp", kt[hh].t[:, :], KaT[hh][:, :], [dres], [kt[hh].r])
                    k.dma("sp", qt[hh].t[:, :], QaT[hh][:, :], [dres], [qt[hh].r])
                    k.dma("sp", vt[hh].t[:, :, 0:128], Va[hh].rearrange("(n p) f -> p n f", p=128), [dres], [vt[hh].r])

                def load_mla(hh):
                    k.dma("sp", kt[hh].t[:, :], KnT[hh][:, :], [dres], [kt[hh].r])
                    k.dma("sp", qt[hh].t[:, :], QnT[hh][:, :], [dres], [qt[hh].r])
                    k.dma("sp", qr[hh].t[:, :], QrT[hh][:, :], [dres], [qr[hh].r])
                    k.dma("sp", vt[hh].t[:, :, 0:128], Vb[hh].rearrange("(n p) f -> p n f", p=128), [dres], [vt[hh].r])

                items = []
                for kind in ("diff", "mla"):
                    for hh in range(2):
                        first_of_head = len(items)
                        for qg in range(8):
                            for m in (range(2) if kind == "diff" else range(1)):
                                ob = [oring.next(), oring.next()]
                                for kb in range(0, 4 * qg + 4):
                                    items.append(dict(kind=kind, hh=hh, qg=qg, m=m, kb=kb, ob=ob, last=(kb == 4 * qg + 3), pre=None))
                        nxt = None
                        if kind == "diff" and hh == 1:
                            nxt = (lambda: load_mla(0))
                        if kind == "mla" and hh == 0:
                            nxt = (lambda: load_mla(1))
                        if nxt is not None:
                            items[first_of_head + 4]["pre"] = nxt
                load_diff(0)
                load_diff(1)

                def geom(it):
                    qg, kb = it["qg"], it["kb"]
                    n0 = max(0, kb - 4 * qg)
                    return n0, (4 - n0) * 128, qg * 512 + n0 * 128

                def s0(it):
                    if it["pre"] is not None:
                        it["pre"]()
                    n0, N, c0 = geom(it)
                    hh, kb = it["hh"], it["kb"]
                    sp = sring.next()
                    it["sp"] = sp
                    kb_, qb_ = kt[hh], qt[hh]
                    if it["kind"] == "diff":
                        pb = it["m"] * 64
                        k.mm(sp.t[:, 0:N], kb_.t[pb:pb + 64, kb * 128:(kb + 1) * 128], qb_.t[pb:pb + 64, c0:c0 + N], True, True,
                             [kb_.r, qb_.r], [sp.r])
                    else:
                        k.mm(sp.t[:, 0:N], kb_.t[:, kb * 128:(kb + 1) * 128], qb_.t[:, c0:c0 + N], True, False, [kb_.r, qb_.r], [sp.r])
                        k.mm(sp.t[:, 0:N], kr.t[:, kb * 128:(kb + 1) * 128], qr[hh].t[:, c0:c0 + N], False, True, [kr.r, qr[hh].r], [sp.r])

                def s1(it):
                    n0, N, c0 = geom(it)
                    hh, kb, qg = it["hh"], it["kb"], it["qg"]
                    sp = it["sp"]
                    pT = pring.next()
                    it["pT"] = pT
                    diag = kb >= 4 * qg
                    if it["kind"] == "mla":
                        scale, dti, mode = MLA_SCALE, 6, "none"
                    elif hh == 0:
                        scale, dti, mode = DIFF_SCALE, n0 % 2, "pair"
                    else:
                        scale, dti, mode = DIFF_SCALE, 2 + n0, "group"
                    if diag:
                        td = tmpd.next()
                        k.stt("dve", td.t[:, :], sp.t[:, 0:128], scale, dtab.t[:, dti, :], ALU.mult, ALU.add, [sp.r, dtab.r], [td.r])
                        k.act(pT.t[:, 0:128], td.t[:, :], AF.Exp, [td.r], [pT.r])
                    lo = 128 if diag else 0
                    if mode == "pair":
                        for p in range(2):
                            ns = [n for n in range(n0 + (1 if diag else 0), 4) if n // 2 == p]
                            if not ns:
                                continue
                            cs = (ns[0] - n0) * 128
                            ce = (ns[-1] - n0 + 1) * 128
                            g = 4 * qg + 2 * p - kb + 1
                            k.act(pT.t[:, cs:ce], sp.t[:, cs:ce], AF.Exp, [sp.r, btab.r], [pT.r],
                                  scale=scale, bias=btab.t[:, 0, g:g + 1])
                    elif N > lo:
                        if mode == "group":
                            g = 4 * qg - kb + 3
                            k.act(pT.t[:, lo:N], sp.t[:, lo:N], AF.Exp, [sp.r, btab.r], [pT.r], scale=scale, bias=btab.t[:, 1, g:g + 1])
                        else:
                            k.act(pT.t[:, lo:N], sp.t[:, lo:N], AF.Exp, [sp.r], [pT.r], scale=scale)

                def normalize(ob, dst_ap, dst_r):
                    for n in range(4):
                        o = ob[n // 2]
                        c = (n % 2) * 129
                        k.recip(rs.t[:, n:n + 1], o.t[:, c + 128:c + 129], [o.r], [rs.r])
                    for n in range(4):
                        o = ob[n // 2]
                        c = (n % 2) * 129
                        k.ts("dve", dst_ap(n), o.t[:, c:c + 128], rs.t[:, n:n + 1], None, ALU.mult, None, [o.r, rs.r], [dst_r])

                def s2(it):
                    n0, N, c0 = geom(it)
                    hh, kb, qg, ob, pT = it["hh"], it["kb"], it["qg"], it["ob"], it["pT"]
                    V = vt[hh]
                    for n in range(n0, 4):
                        cs = (n - n0) * 128
                        o = ob[n // 2]
                        k.mm(o.t[:, (n % 2) * 129:(n % 2) * 129 + 129], pT.t[:, cs:cs + 128], V.t[:, kb, :],
                             kb == 0 and n % 2 == 0, kb == 4 * qg + n, [pT.r, V.r], [o.r])
                    if not it["last"]:
                        return
                    osl = Osc[0].rearrange("(n p) f -> p n f", p=128)
                    if it["kind"] == "mla":
                        ob_ = obf.next()
                        normalize(ob, lambda n: ob_.t[:, n, :], ob_.r)
                        k.dma("sp", osl[:, 4 * qg:4 * qg + 4, 256 + hh * 128:256 + (hh + 1) * 128], ob_.t[:, :, :], [ob_.r], [dres])
                        return
                    tgt = o1n if it["m"] == 0 else o2n
                    normalize(ob, lambda n: tgt.t[:, n, :], tgt.r)
                    if it["m"] == 0:
                        return
                    k.stt("dve", od.t[:, :, :], o2n.t[:, :, :], lam.t[:, 0:1], o1n.t[:, :, :], ALU.mult, ALU.add,
                          [o2n.r, o1n.r, lam.r], [od.r])
                    k.memset("dve", ss2.t[:, 0:4], 0.0, [ss2.r])
                    for n in range(4):
                        k.act(junk.t[:, 0:128], od.t[:, n, :], AF.Square, [od.r, ss2.r], [junk.r, ss2.r], accum_out=ss2.t[:, n:n + 1])
                    k.rstd_chain(ss2, rstd2, 4, 1.0 / 128)
                    ob_ = obf.next()
                    for n in range(4):
                        k.stt("dve", ob_.t[:, n, :], od.t[:, n, :], rstd2.t[:, n:n + 1], gsub.t[:, :], ALU.mult, ALU.mult,
                              [od.r, rstd2.r, gsub.r], [ob_.r])
                    k.dma("sp", osl[:, 4 * qg:4 * qg + 4, hh * 128:(hh + 1) * 128], ob_.t[:, :, :], [ob_.r], [dres])

                pipeline(items, [s0, s1, s2])
                P.flush()

        def phase_X(pairs, pre=None):
            if pre is not None:
                pre()
            for src, dst in pairs:
                P.add("pool", lambda e, src=src, dst=dst: e.collective_compute("AllGather", ALU.bypass, replica_groups=RG, ins=[src], outs=[dst]),
                      [dres], [dres], cc=True)
            P.flush()

        def phase_M(layer, kxT, vx, wkv):
            with ExitStack() as s:
                gm = k.sb(s, "gm", [128, D], F32)
                mt = k.sb(s, "mt", [128, 2, D], F32)
                mn = k.sb(s, "mn", [128, 2, D], BF16)
                mT = k.sb(s, "mT", [128, 8, 256], BF16)
                psr = Ring([k.ps(s, "psm%d" % i) for i in range(6)])
                k.dma("sp", gm.t[:, :], I["gains"][4 + layer], [], [gm.r])
                k.dma("sp", mt.t[:, :, :], I["mem"].rearrange("(n p) f -> p n f", p=128), [], [mt.r])
                k.rmsnorm(lambda j: mt.t[:, j, :], mt.r, 2, gm, lambda j: mn.t[:, j, :], mn.r, junk, ss, rstd)
                k.transpose(lambda j, c: mn.t[:, j, c * 128:(c + 1) * 128], [mn.r], 2, 8, identb, lambda c: mT.t[:, c, :], mT.r, psr)
                for cc in range(8):
                    ps = psr.next()
                    for kk in range(8):
                        k.mm(ps.t[:, 0:256], wkv.t[:, kk, cc * 128:(cc + 1) * 128], mT.t[:, kk, :], kk == 0, kk == 7, [wkv.r, mT.r], [ps.r])
                    k.evac(kxT.t[:, cc, :], ps.t[:, 0:256], [ps.r], [kxT.r])
                k.memset("pool", vx.t[:, :, :, :], 1.0, [vx.r])
                for mb in range(2):
                    for half in range(2):
                        ps = psr.next()
                        for kk in range(8):
                            k.mm(ps.t[:, :], mT.t[:, kk, mb * 128:(mb + 1) * 128], wkv.t[:, kk, D + half * 512:D + (half + 1) * 512],
                                 kk == 0, kk == 7, [wkv.r, mT.r], [ps.r])
                        for hx in range(2):
                            k.evac(vx.t[:, mb, half * 2 + hx, 0:256], ps.t[:, hx * 256:(hx + 1) * 256], [ps.r], [vx.r])
                P.flush()

        def phase_Ra(layer, Ogath, chunk_map, wkv, free_wkv, wout, wq, wo):
            with ExitStack() as s:
                kxT = k.sb(s, "kxT", [128, 8, 256], BF16)
                vx = k.sb(s, "vx", [128, 2, 4, 257], BF16)
                phase_M(layer, kxT, vx, wkv)
                free_wkv()
                gx = k.sb(s, "gx", [128, D], F32)
                ol = [k.sb(s, "ol%d" % hf, [128, 4, 512], BF16) for hf in range(2)]
                oT = k.sb(s, "oT", [128, 8, 512], BF16)
                hn = k.sb(s, "hn", [128, 4, D], BF16)
                hnT = [k.sb(s, "hnT%d" % i, [128, 8, 512], BF16) for i in range(2)]
                qx = Ring([k.sb(s, "qx%d" % i, [128, 2, 512], BF16) for i in range(2)])
                pring = Ring([k.sb(s, "pTx%d" % i, [128, 512], BF16) for i in range(4)])
                oxn = [k.sb(s, "oxn%d" % i, [128, 4, D], BF16) for i in range(2)]
                oxT = k.sb(s, "oxT", [128, 8, 512], BF16)
                rs = k.sb(s, "rsx", [128, 4], F32)
                ssx = k.sb(s, "ssx", [128, 4], F32)
                rstdx = k.sb(s, "rstdx", [128, 4], F32)
                psr = Ring([k.ps(s, "psa%d" % i) for i in range(5)])
                oring = Ring([k.ps(s, "psox%d" % i) for i in range(3)])
                k.dma("sp", gx.t[:, :], I["gains"][2 + layer], [], [gx.r])

                def proj_add(t, srcT, w):
                    for j in range(4):
                        for half in range(2):
                            ps = psr.next()
                            for kk in range(8):
                                k.mm(ps.t[:, :], srcT.t[:, kk, j * 128:(j + 1) * 128], w.t[:, kk, half * 512:(half + 1) * 512],
                                     kk == 0, kk == 7, [srcT.r, w.r], [ps.r])
                            hap = h.t[:, t * 4 + j, half * 512:(half + 1) * 512]
                            hres = hr[t * 4 + j][half]
                            k.tt("dve", hap, ps.t[:, :], hap, ALU.add, [ps.r, hres], [hres])

                def sA(t):
                    for r in range(2):
                        for hf in range(2):
                            row0 = r * OWN + t * 512
                            k.dma("sp", ol[hf].t[:, :, :], Ogath[hf][row0:row0 + 512, :].rearrange("(n p) f -> p n f", p=128),
                                  [dres], [ol[hf].r])
                        for c4 in range(4):
                            ps = psr.next()
                            for j in range(4):
                                for hf in range(2):
                                    k.mm(ps.t[:, j * 128:(j + 1) * 128], ol[hf].t[:, j, c4 * 128:(c4 + 1) * 128], cm.t[:, 1 + hf, :],
                                         hf == 0, hf == 1, [ol[hf].r, cm.r], [ps.r])
                            k.evac(oT.t[:, chunk_map(r, c4), :], ps.t[:, :], [ps.r], [oT.r])
                    proj_add(t, oT, wout)
                    X = hnT[t % 2]
                    k.rmsnorm(lambda j: h.t[:, t * 4 + j, :], lambda j: list(hr[t * 4 + j]), 4, gx, lambda j: hn.t[:, j, :], hn.r, junk, ssx, rstdx)
                    k.transpose(lambda j, c: hn.t[:, j, c * 128:(c + 1) * 128], [hn.r], 4, 8, identb, lambda c: X.t[:, c, :], X.r, psr)

                def sB(t):
                    X = hnT[t % 2]
                    ox = oxn[t % 2]
                    for hx in range(4):
                        q = qx.next()
                        for c2 in range(2):
                            ps = psr.next()
                            c0 = hx * 256 + c2 * 128
                            for kk in range(8):
                                k.mm(ps.t[:, :], wq.t[:, kk, c0:c0 + 128], X.t[:, kk, :], kk == 0, kk == 7, [wq.r, X.r], [ps.r])
                            k.evac(q.t[:, c2, :], ps.t[:, :], [ps.r], [q.r])
                        ob = [oring.next(), oring.next()]
                        pts = []
                        for mb in range(2):
                            ps = psr.next()
                            for c2 in range(2):
                                k.mm(ps.t[:, :], kxT.t[:, hx * 2 + c2, mb * 128:(mb + 1) * 128], q.t[:, c2, :], c2 == 0, c2 == 1,
                                     [kxT.r, q.r], [ps.r])
                            pT = pring.next()
                            k.act(pT.t[:, :], ps.t[:, :], AF.Exp, [ps.r], [pT.r], scale=XA_SCALE)
                            pts.append(pT)
                        for j in range(4):
                            o = ob[j // 2]
                            for mb in range(2):
                                k.mm(o.t[:, (j % 2) * 256:(j % 2) * 256 + 256], pts[mb].t[:, j * 128:(j + 1) * 128], vx.t[:, mb, hx, 0:256],
                                     mb == 0, mb == 1, [pts[mb].r, vx.r], [o.r])
                        sb_ = oring.next()
                        for j in range(4):
                            for mb in range(2):
                                k.mm(sb_.t[:, j:j + 1], pts[mb].t[:, j * 128:(j + 1) * 128], vx.t[:, mb, hx, 256:257],
                                     mb == 0, mb == 1, [pts[mb].r, vx.r], [sb_.r])
                        k.recip(rs.t[:, 0:4], sb_.t[:, 0:4], [sb_.r], [rs.r])
                        for j in range(4):
                            o = ob[j // 2]
                            k.ts("dve", ox.t[:, j, hx * 256:(hx + 1) * 256], o.t[:, (j % 2) * 256:(j % 2) * 256 + 256], rs.t[:, j:j + 1], None,
                                 ALU.mult, None, [o.r, rs.r], [ox.r])

                def sC(t):
                    ox = oxn[t % 2]
                    k.transpose(lambda j, c: ox.t[:, j, c * 128:(c + 1) * 128], [ox.r], 4, 8, identb, lambda c: oxT.t[:, c, :], oxT.r, psr)
                    proj_add(t, oxT, wo)

                pipeline(list(range(4)), [sA, sB, sC])
                P.flush()

        def phase_Rb(layer):
            with ExitStack() as s:
                gm = k.sb(s, "gmlp", [128, D], F32)
                hn = k.sb(s, "hnm", [128, 4, D], BF16)
                hnT = k.sb(s, "hnTm", [128, 8, OWN], BF16)
                w1g = [k.sb(s, "w1g%d" % i, [128, 8, 512], BF16) for i in range(2)]
                w2g = [k.sb(s, "w2g%d" % i, [128, 4, D], BF16) for i in range(2)]
                uT = [k.sb(s, "uT%d" % i, [128, 4, OWN], BF16) for i in range(2)]
                rl = Ring([k.sb(s, "rl%d" % i, [128, 512], F32) for i in range(3)])
                psr = Ring([k.ps(s, "psb%d" % i) for i in range(4)])
                psy = Ring([k.ps(s, "psy%d" % i) for i in range(4)])
                hnT_r = [Res("hnT%d" % i) for i in range(4)]
                k.dma("sp", gm.t[:, :], I["gains"][6 + layer], [], [gm.r])
                for t in range(4):
                    k.rmsnorm(lambda j: h.t[:, t * 4 + j, :], lambda j: list(hr[t * 4 + j]), 4, gm, lambda j: hn.t[:, j, :], hn.r, junk, ss, rstd)
                    k.transpose(lambda j, c: hn.t[:, j, c * 128:(c + 1) * 128], [hn.r], 4, 8, identb,
                                lambda c: hnT.t[:, c, t * 512:(t + 1) * 512], hnT_r[t], psr)
                for fg in range(8):
                    w1 = w1g[fg % 2]
                    w2 = w2g[fg % 2]
                    u = uT[fg % 2]
                    k.dma("pool", w1.t[:, :, :], I["w1"][layer][:, fg * 512:(fg + 1) * 512].rearrange("(k p) c -> p k c", p=128), [], [w1.r])
                    k.dma("pool", w2.t[:, :, :], I["w2"][layer][fg * 512:(fg + 1) * 512, :].rearrange("(k p) c -> p k c", p=128), [], [w2.r])
                    for t in range(4):
                        for fc in range(4):
                            ps = psr.next()
                            for kk in range(8):
                                k.mm(ps.t[:, :], w1.t[:, kk, fc * 128:(fc + 1) * 128], hnT.t[:, kk, t * 512:(t + 1) * 512], kk == 0, kk == 7,
                                     [w1.r, hnT_r[t]], [ps.r])
                            r_ = rl.next()
                            k.act(r_.t[:, :], ps.t[:, :], AF.Relu, [ps.r], [r_.r])
                            k.tt("pool", u.t[:, fc, t * 512:(t + 1) * 512], r_.t[:, :], r_.t[:, :], ALU.mult, [r_.r], [u.r])
                    for j in range(16):
                        for half in range(2):
                            ps = psy.next()
                            for fc in range(4):
                                k.mm(ps.t[:, :], u.t[:, fc, j * 128:(j + 1) * 128], w2.t[:, fc, half * 512:(half + 1) * 512], fc == 0, fc == 3,
                                     [u.r, w2.r], [ps.r])
                            hap = h.t[:, j, half * 512:(half + 1) * 512]
                            k.tt("dve", hap, ps.t[:, :], hap, ALU.add, [ps.r, hr[j][half]], [hr[j][half]])
                P.flush()

        def phase_E1():
            with ExitStack() as s:
                g = k.sb(s, "god", [128, D], F32)
                hb = k.sb(s, "hb", [128, 16, D], BF16)
                k.dma("sp", g.t[:, :], I["gains"][1], [], [g.r])
                for t in range(4):
                    k.rmsnorm(lambda j: h.t[:, t * 4 + j, :], lambda j: list(hr[t * 4 + j]), 4, g, lambda j: hb.t[:, t * 4 + j, :], hb.r, junk, ss, rstd)
                k.dma("sp", Hn.rearrange("(n p) f -> p n f", p=128), hb.t[:, :, :], [hb.r], [dres])
                P.flush()

        def phase_P1():
            with ExitStack() as s:
                wsb = k.sb(s, "wsb", [128, 8, 1536], BF16)
                xn = [k.sb(s, "xn1%d" % i, [128, 4, D], BF16) for i in range(2)]
                xnT = [k.sb(s, "xnT1%d" % i, [128, 8, 512], BF16) for i in range(2)]
                osb = Ring([k.sb(s, "osb1%d" % i, [128, 512], BF16) for i in range(4)])
                vsb = Ring([k.sb(s, "vsb1%d" % i, [128, 4, 512], BF16) for i in range(2)])
                psr = Ring([k.ps(s, "psp%d" % i) for i in range(8)])
                wsb_r = [Res("wsbQ"), Res("wsbK"), Res("wsbV")]
                for gi_ in range(3):
                    k.dma("pool", wsb.t[:, :, gi_ * 512:(gi_ + 1) * 512], I["w_sb"][:, gi_ * 512:(gi_ + 1) * 512].rearrange("(k p) c -> p k c", p=128),
                          [], [wsb_r[gi_]])

                def sA(t):
                    xb = xn[t % 2]
                    X = xnT[t % 2]
                    hrow = (t // 4) * 1024 + ((t % 4) % 2) * 512
                    k.dma("sp", xb.t[:, :, :], Hg[(t % 4) // 2][hrow:hrow + 512, :].rearrange("(n p) f -> p n f", p=128), [dres], [xb.r])
                    k.transpose(lambda j, c: xb.t[:, j, c * 128:(c + 1) * 128], [xb.r], 4, 8, identb, lambda c: X.t[:, c, :], X.r, psr)

                def sB(t):
                    tok = slice(t * 512, (t + 1) * 512)
                    X = xnT[t % 2]
                    for ci in range(8):
                        ps = psr.next()
                        for kk in range(8):
                            k.mm(ps.t[:, :], wsb.t[:, kk, ci * 128:(ci + 1) * 128], X.t[:, kk, :], kk == 0, kk == 7, [wsb_r[ci // 4], X.r], [ps.r])
                        o = osb.next()
                        if ci < 4:
                            k.act(o.t[:, :], ps.t[:, :], AF.Copy, [ps.r], [o.r], scale=0.125)
                            k.dma("sp", QsT[ci][:, tok], o.t[:, :], [o.r], [dres])
                        else:
                            k.evac(o.t[:, :], ps.t[:, :], [ps.r], [o.r])
                            k.dma("sp", KsT[ci - 4][:, tok], o.t[:, :], [o.r], [dres])
                    vo = vsb.next()
                    for j in range(4):
                        ps = psr.next()
                        for kk in range(8):
                            k.mm(ps.t[:, :], X.t[:, kk, j * 128:(j + 1) * 128], wsb.t[:, kk, 1024:1536], kk == 0, kk == 7, [wsb_r[2], X.r], [ps.r])
                        k.evac(vo.t[:, j, :], ps.t[:, :], [ps.r], [vo.r])
                    k.dma("sp", Vs[tok, :].rearrange("(n p) f -> p n f", p=128), vo.t[:, :, :], [vo.r], [dres])

                pipeline(list(range(8)), [sA, sB])
                P.flush()

        def phase_A1(pre=None):
            with ExitStack() as s:
                if pre is not None:
                    pre()
                kt = [k.sb(s, "skt%d" % i, [128, S], BF16) for i in range(2)]
                qt = [k.sb(s, "sqt%d" % i, [128, S], BF16) for i in range(2)]
                vt = k.sb(s, "svt", [128, NB, 512], BF16)
                esb = Ring([k.sb(s, "esb%d" % i, [128, 512], F32) for i in range(2)])
                spb = Ring([k.sb(s, "spb%d" % i, [128, 512], BF16) for i in range(3)])
                Rring = Ring([k.sb(s, "Rb%d" % i, [128, 512], BF16) for i in range(3)])
                pring = Ring([k.sb(s, "spT%d" % i, [128, 512], BF16) for i in range(3)])
                obf = Ring([k.sb(s, "sobf%d" % i, [128, 4, 64], BF16) for i in range(2)])
                zring = Ring([k.ps(s, "zps%d" % i) for i in range(3)])
                aring = Ring([k.ps(s, "aps%d" % i) for i in range(3)])
                oring = Ring([k.ps(s, "sops%d" % i) for i in range(2)])
                trineg = cm.t[:, 3, :]
                negones = cm.t[:, 4, :]
                sbmask = cm.t[:, 5, :]
                k.dma("sp", vt.t[:, :, :], Vs.rearrange("(n p) f -> p n f", p=128), [dres], [vt.r])

                def load_pair(hp):
                    k.dma("sp", kt[hp % 2].t[:, :], KsT[hp][:, :], [dres], [kt[hp % 2].r])
                    k.dma("sp", qt[hp % 2].t[:, :], QsT[hp][:, :], [dres], [qt[hp % 2].r])

                items = []
                for hp in range(4):
                    first_of_pair = len(items)
                    for hl in range(2):
                        for qg in range(8):
                            o = oring.next()
                            prev = None
                            for ui, kb in enumerate(range(4 * qg + 3, -1, -1)):
                                it = dict(hp=hp, hl=hl, qg=qg, kb=kb, ui=ui, o=o, pre=None)
                                it["Rin"] = prev["Rout"] if prev is not None else None
                                it["Rout"] = Rring.next() if kb > 0 else None
                                items.append(it)
                                prev = it
                    if hp + 1 < 4:
                        items[first_of_pair + 6]["pre"] = (lambda hp=hp: load_pair(hp + 1))
                load_pair(0)

                def geom(it):
                    qg, kb = it["qg"], it["kb"]
                    n0 = max(0, kb - 4 * qg)
                    return n0, (4 - n0) * 128, qg * 512 + n0 * 128

                def qk(it, dst, start, stop):
                    n0, N, c0 = geom(it)
                    kb = it["kb"]
                    pb = it["hl"] * 64
                    kb_, qb_ = kt[it["hp"] % 2], qt[it["hp"] % 2]
                    k.mm(dst.t[:, 0:N], kb_.t[pb:pb + 64, kb * 128:(kb + 1) * 128], qb_.t[pb:pb + 64, c0:c0 + N], start, stop,
                         [kb_.r, qb_.r], [dst.r])

                def s0(it):
                    if it["pre"] is not None:
                        it["pre"]()
                    it["zp"] = zring.next()
                    qk(it, it["zp"], True, True)

                def s1(it):
                    n0, N, c0 = geom(it)
                    off = n0 * 128
                    diag = it["kb"] >= 4 * it["qg"]
                    zp = it["zp"]
                    e_ = esb.next()
                    k.act(e_.t[:, 0:N], zp.t[:, 0:N], AF.Exp, [zp.r], [e_.r])
                    sp_ = spb.next()
                    it["sp"] = sp_
                    k.act(sp_.t[:, 0:N], e_.t[:, 0:N], AF.Ln, [e_.r], [sp_.r], bias=1.0)
                    if diag:
                        k.tt("pool", sp_.t[:, 0:128], sp_.t[:, 0:128], sbmask, ALU.mult, [sp_.r, cm.r], [sp_.r])
                    Rn, Rin = it["Rout"], it["Rin"]
                    if Rn is not None:
                        if Rin is None:
                            k.memset("pool", Rn.t[:, :], 0.0, [Rn.r])
                            k.copy("pool", Rn.t[:, off:off + N], sp_.t[:, 0:N], [sp_.r], [Rn.r])
                        else:
                            if off > 0:
                                k.memset("pool", Rn.t[:, 0:off], 0.0, [Rn.r])
                            k.tt("dve", Rn.t[:, off:off + N], Rin.t[:, off:off + N], sp_.t[:, 0:N], ALU.add, [Rin.r, sp_.r], [Rn.r])

                def s2(it):
                    n0, N, c0 = geom(it)
                    off = n0 * 128
                    ap_ = aring.next()
                    it["ap"] = ap_
                    sp_, Rin = it["sp"], it["Rin"]
                    qk(it, ap_, True, False)
                    k.mm(ap_.t[:, 0:N], trineg, sp_.t[:, 0:N], False, Rin is None, [cm.r, sp_.r], [ap_.r])
                    if Rin is not None:
                        k.mm(ap_.t[:, 0:N], negones, Rin.t[:, off:off + N], False, True, [cm.r, Rin.r], [ap_.r])

                def s3(it):
                    n0, N, c0 = geom(it)
                    diag = it["kb"] >= 4 * it["qg"]
                    ap_ = it["ap"]
                    pT = pring.next()
                    it["pT"] = pT
                    k.act(pT.t[:, 0:N], ap_.t[:, 0:N], AF.Exp, [ap_.r], [pT.r])
                    if diag:
                        k.tt("pool", pT.t[:, 0:128], pT.t[:, 0:128], sbmask, ALU.mult, [pT.r, cm.r], [pT.r])

                def s4(it):
                    n0, N, c0 = geom(it)
                    kb, qg, o, pT = it["kb"], it["qg"], it["o"], it["pT"]
                    hd = it["hp"] * 2 + it["hl"]
                    for n in range(n0, 4):
                        cs = (n - n0) * 128
                        k.mm(o.t[:, n * 64:(n + 1) * 64], pT.t[:, cs:cs + 128], vt.t[:, kb, hd * 64:(hd + 1) * 64],
                             it["ui"] == 0, kb == 0, [pT.r, vt.r], [o.r])
                    if kb == 0:
                        ob_ = obf.next()
                        k.evac(ob_.t[:, :, :], o.t[:, 0:256], [o.r], [ob_.r])
                        k.dma("sp", Osc[1].rearrange("(n p) f -> p n f", p=128)[:, 4 * qg:4 * qg + 4, hd * 64:(hd + 1) * 64],
                              ob_.t[:, :, :], [ob_.r], [dres])

                pipeline(items, [s0, s1, s2, s3, s4])
                P.flush()

        def phase_F():
            with ExitStack() as s:
                g = k.sb(s, "gfin", [128, D], F32)
                yb = k.sb(s, "yb", [128, 4, D], F32)
                k.dma("sp", g.t[:, :], I["gains"][8], [], [g.r])
                for t in range(4):
                    k.rmsnorm(lambda j: h.t[:, t * 4 + j, :], lambda j: list(hr[t * 4 + j]), 4, g, lambda j: yb.t[:, j, :], yb.r, junk, ss, rstd)
                    k.dma("sp", y_out[t * 512:(t + 1) * 512, :].rearrange("(n p) f -> p n f", p=128), yb.t[:, :, :], [yb.r], [dres])
                P.flush()

        def dump_h():
            with ExitStack() as s:
                k.dma("sp", y_out.rearrange("(n p) f -> p n f", p=128), h.t[:, :, :], hall, [dres])
                P.flush()

        def mixer(layer):
            w_o_in = I["w_out0"] if layer == 0 else I["w_sbo"]
            cmap = (lambda r, c4: (c4 if c4 < 2 else 2 + c4) + 2 * r) if layer == 0 else (lambda r, c4: r * 4 + c4)
            with ExitStack() as sx, ExitStack() as sr:
                wkv = k.sb(sr, "wkv", [128, 8, 2 * D], BF16, side="right")

                def pre_a():
                    for half in range(2):
                        k.dma("pool", wkv.t[:, :, half * D:(half + 1) * D],
                              I["xa_wkv"][layer][:, half * D:(half + 1) * D].rearrange("(k p) c -> p k c", p=128), [], [wkv.r])

                if layer == 0:
                    phase_A0(pre_a)
                else:
                    phase_A1(pre_a)
                wout = k.sb(sx, "wout", [128, 8, D], BF16)
                wq = k.sb(sx, "wq", [128, 8, D], BF16)
                wo = k.sb(sx, "wo", [128, 8, D], BF16)

                def pre_x():
                    k.dma("pool", wout.t[:, :, :], w_o_in.rearrange("(k p) c -> p k c", p=128), [], [wout.r])
                    k.dma("pool", wq.t[:, :, :], I["xa_wq"][layer].rearrange("(k p) c -> p k c", p=128), [], [wq.r])
                    k.dma("pool", wo.t[:, :, :], I["xa_wo"][layer].rearrange("(k p) c -> p k c", p=128), [], [wo.r])

                phase_X([(Osc[layer][q * OWN:(q + 1) * OWN, :], Og[layer][q]) for q in range(2)], pre_x)
                phase_Ra(layer, Og[layer], cmap, wkv, sr.close, wout, wq, wo)

        stages = [
            ("P0", phase_P0),
            ("Ra0", lambda: mixer(0)),
            ("Rb0", lambda: phase_Rb(0)),
            ("E1", phase_E1),
            ("XH", lambda: phase_X([(Hn[q * 1024:(q + 1) * 1024, :], Hg[q]) for q in range(2)])),
            ("P1", phase_P1),
            ("Ra1", lambda: mixer(1)),
            ("Rb1", lambda: phase_Rb(1)),
        ]
        done = False
        for name, fn in stages:
            fn()
            if stop_after == name:
                dump_h()
                done = True
                break
        if not done:
            phase_F()
    return nc


def _consts(c):
    ident = np.eye(128, dtype=np.float32)
    sel0 = ident * (1.0 if c == 0 else 0.0)
    sel1 = ident * (1.0 if c == 1 else 0.0)
    jj = np.arange(128)[:, None]
    ii = np.arange(128)[None, :]
    trineg = np.where(jj >= ii, -1.0, 0.0).astype(np.float32)
    negones = -np.ones((128, 128), np.float32)
    sbmask = (jj < ii).astype(np.float32)
    ones = np.ones((128, 128), np.float32)
    cmat = np.stack([ident, sel0, sel1, trineg, negones, sbmask, ones]).astype(np.float32)
    half = 32
    inv = (10000.0 ** (-np.arange(half, dtype=np.float32) / half)).astype(np.float32)
    ang = (np.arange(S, dtype=np.float32)[:, None] * inv[None, :]).astype(np.float32)
    cos = np.cos(ang).astype(np.float32).T
    sin = np.sin(ang).astype(np.float32).T
    rope = np.stack([np.concatenate([cos, cos], 0), np.concatenate([-sin, sin], 0)]).astype(np.float32)
    slopes = 2.0 ** (-8.0 * np.arange(1, 5, dtype=np.float32) / 4)
    btab = np.zeros((2, 128, 32), np.float32)
    dtab = np.zeros((7, 128, 128), np.float32)
    chunkmask = (jj >= 64) & (ii < 64)
    jf = np.arange(128, dtype=np.float32)[:, None]
    gi = np.arange(32, dtype=np.float32)[None, :]
    iif = ii.astype(np.float32)
    adist = np.abs(ii - jj).astype(np.float32)
    sl0 = slopes[2 * c]
    btab[0] = sl0 * (jf - 128.0 * (gi - 1.0) - 64.0)
    for par in range(2):
        dtab[par] = np.where(chunkmask, NEG, -sl0 * adist + sl0 * iif + sl0 * (128.0 * par - 64.0))
    sl1 = slopes[2 * c + 1]
    btab[1] = sl1 * (jf - 128.0 * (gi - 3.0) - 192.0)
    for n in range(4):
        dtab[2 + n] = np.where(chunkmask, NEG, -sl1 * adist + sl1 * iif + sl1 * (128.0 * n - 192.0))
    dtab[6] = np.where(chunkmask, NEG, 0.0)
    return cmat, rope, btab, dtab


def _core_inputs(b, c, a):
    f = lambda v: np.ascontiguousarray(v, dtype=np.float32)
    rep = lambda v, n=128: np.ascontiguousarray(np.broadcast_to(np.asarray(v, np.float32).reshape(1, -1), (n, np.asarray(v).size)))
    x = a["x"][b]
    cmat, rope, btab, dtab = _consts(c)
    w_in = a["ev_w_in"][0]
    dh = [2 * c, 2 * c + 1]
    cols = []
    for hh in dh:
        cols.append(np.arange(hh * 128, (hh + 1) * 128))
    for hh in dh:
        cols.append(512 + np.arange(hh * 128, (hh + 1) * 128))
    for hh in dh:
        cols.append(1024 + np.arange(hh * 128, (hh + 1) * 128))
    cols.append(np.arange(1536, 1792))
    cols.append(np.arange(1792, 1920))
    cols.append(np.arange(1920, 1984))
    cols.append(np.concatenate([np.arange(1952, 1984), np.arange(1920, 1952)]))
    w_in0 = w_in[:, np.concatenate(cols)]
    w_uq = a["mla_w_uq"][0]
    ucols = []
    for hh in dh:
        ucols.append(np.arange(hh * 192, hh * 192 + 128))
    for hh in dh:
        base = hh * 192 + 128
        ucols.append(np.arange(base, base + 64))
        ucols.append(np.concatenate([np.arange(base + 32, base + 64), np.arange(base, base + 32)]))
    w_uq_o = w_uq[:, np.concatenate(ucols)]
    w_ukv = a["mla_w_ukv"][0]
    kcols = []
    for hh in dh:
        kcols.append(np.arange(hh * 256, hh * 256 + 128))
    for hh in dh:
        kcols.append(np.arange(hh * 256 + 128, hh * 256 + 256))
    w_ukv_o = w_ukv[:, np.concatenate(kcols)]
    sbw = a["sb_w_in"][0]
    sh = np.arange(c * 512, (c + 1) * 512)
    w_sb = sbw[:, np.concatenate([sh, 1024 + sh, 2048 + sh])]
    gains = np.stack([rep(a["ev_norm"][0]), rep(a["od_norm"][0]), rep(a["xa_norm"][0]), rep(a["xa_norm"][1]),
                      rep(a["xa_mem_norm"][0]), rep(a["xa_mem_norm"][1]), rep(a["mlp_norm"][0]), rep(a["mlp_norm"][1]),
                      rep(a["final_norm"])])
    lamv = np.stack([rep(a["diff_lq1"][0]), rep(a["diff_lk1"][0]), rep(a["diff_lq2"][0]), rep(a["diff_lk2"][0])], axis=1)
    return {
        "x_full": f(x), "x_own": f(x[c * OWN:(c + 1) * OWN]), "mem": f(a["mem"][b]), "gains": f(gains),
        "w_in0": f(w_in0), "w_uq": f(w_uq_o), "w_ukv": f(w_ukv_o), "w_out0": f(a["ev_w_out"][0]),
        "g_cq": f(np.asarray(a["mla_g_cq"][0]).reshape(2, 128).T), "g_ckv": f(np.asarray(a["mla_g_ckv"][0]).reshape(128, 1)),
        "g_sub": f(rep(a["diff_subln"][0])), "lamv": f(lamv),
        "w_sb": f(w_sb), "w_sbo": f(a["sb_w_out"][0]),
        "xa_wq": f(a["xa_wq"]), "xa_wkv": f(a["xa_wkv"]), "xa_wo": f(a["xa_wo"]), "w1": f(a["mlp_w1"]), "w2": f(a["mlp_w2"]),
        "cmat": f(cmat), "rope": f(rope), "btab": f(btab), "dtab": f(dtab),
    }


_NC_CACHE = {}


def run(inputs, stop_after=None):
    a = {k_: np.asarray(v) for k_, v in inputs.items()}
    if stop_after not in _NC_CACHE:
        _NC_CACHE[stop_after] = build_program(stop_after)
    nc = _NC_CACHE[stop_after]
    in_maps = [_core_inputs(core // 2, core % 2, a) for core in range(8)]
    res = run_bass_kernel_spmd(nc, in_maps, core_ids=list(range(8)))
    out = np.empty((4, S, D), np.float32)
    for core in range(8):
        b, c = core // 2, core % 2
        out[b, c * OWN:(c + 1) * OWN] = res.results[core]["y"]
    return out


def kernel(**inputs):
    return run(inputs)
```

```python
import math
from contextlib import ExitStack

import numpy as np
import concourse.bass as bass
import concourse.mybir as mybir
from concourse.bass_utils import run_bass_kernel_spmd

F32 = mybir.dt.float32
BF16 = mybir.dt.bfloat16
AF = mybir.ActivationFunctionType
ALU = mybir.AluOpType

S = 4096
D = 1024
OWN = 2048
NB = 32
EPS = 1e-6
DIFF_SCALE = 64 ** -0.5
MLA_SCALE = 192 ** -0.5
XA_SCALE = 256 ** -0.5
LAMBDA_INIT0 = 0.8 - 0.6 * math.exp(-0.3 * 0)
NEG = -30000.0

ENGS = ("pe", "act", "dve", "pool", "sp")
DMAQ = ("sp", "pool")


class Res:
    __slots__ = ("name", "w", "rs")

    def __init__(self, name=""):
        self.name = name
        self.w = None
        self.rs = []


class Op:
    __slots__ = ("eng", "fn", "need", "signal", "cnt", "dma", "sem", "semval", "phase", "inc")

    def __init__(self, eng, fn, dma, phase):
        self.eng = eng
        self.fn = fn
        self.need = []
        self.signal = False
        self.cnt = 0
        self.dma = dma
        self.sem = None
        self.semval = 0
        self.phase = phase
        self.inc = 16


class Prog:
    NRING = {"sp": 12, "pool": 6}

    def __init__(self, nc, stack):
        self.nc = nc
        self.phase = 0
        self.ops = {e: [] for e in ENGS}
        self.csem = {}
        self.ccnt = {}
        for e in ("pe", "act", "dve", "pool"):
            self.csem[e] = stack.enter_context(nc.semaphore("c_" + e))
            self.ccnt[e] = 0
        self.ring = {}
        self.ringval = {}
        self.dk = {}
        for q in DMAQ:
            self.ring[q] = [stack.enter_context(nc.semaphore("d_%s%d" % (q, i))) for i in range(self.NRING[q])]
            self.ringval[q] = [0] * self.NRING[q]
            self.dk[q] = 0
        self.ccsem = stack.enter_context(nc.semaphore("ccsem"))
        self.ccval = 0
        self.waited = {e: {} for e in ENGS}
        self.nops = 0

    def add(self, eng, fn, reads=(), writes=(), dma=False, cc=False):
        op = Op(eng, fn, dma or cc, self.phase)
        raw = set()
        other = set()
        for r in reads:
            if r.w is not None:
                raw.add(r.w)
        for r in writes:
            if r.w is not None:
                other.add(r.w)
            for o in r.rs:
                other.add(o)
        for d in raw | other:
            if d.phase != self.phase or d is op:
                continue
            if not d.dma and d.eng == eng:
                if eng == "pe":
                    continue
                if d not in raw:
                    continue
            op.need.append(d)
        if cc:
            self.ccval += 1
            op.sem = self.ccsem
            op.semval = self.ccval
            op.inc = 1
        elif dma:
            q = eng
            i = self.dk[q] % self.NRING[q]
            self.dk[q] += 1
            self.ringval[q][i] += 16
            op.sem = self.ring[q][i]
            op.semval = self.ringval[q][i]
        for r in reads:
            r.rs.append(op)
        for r in writes:
            r.w = op
            r.rs = []
        self.ops[eng].append(op)
        self.nops += 1
        return op

    def flush(self):
        nc = self.nc
        lasts = {}
        for e in ("pe", "act", "dve", "pool"):
            for op in reversed(self.ops[e]):
                if not op.dma:
                    lasts[e] = op
                    break
        for e in ENGS:
            for op in self.ops[e]:
                for d in op.need:
                    d.signal = True
        for op in lasts.values():
            op.signal = True
        for e in ("pe", "act", "dve", "pool"):
            c = self.ccnt[e]
            for op in self.ops[e]:
                if op.signal and not op.dma:
                    c += 1
                    op.cnt = c
            self.ccnt[e] = c
        ringfinal = {q: list(self.ringval[q]) for q in DMAQ}
        ccfinal = self.ccval

        def emit(ename, eng):
            w = self.waited[ename]

            def wait(sem, key, val):
                if w.get(key, 0) < val:
                    eng.wait_ge(sem, val)
                    w[key] = val

            for op in self.ops[ename]:
                for d in op.need:
                    if d.dma:
                        wait(d.sem, id(d.sem), d.semval)
                    else:
                        wait(self.csem[d.eng], d.eng, d.cnt)
                if op.dma and op.inc == 16 and op.semval > 16:
                    wait(op.sem, id(op.sem), op.semval - 16)
                ins = op.fn(eng)
                if op.dma:
                    ins.then_inc(op.sem, op.inc)
                elif op.signal:
                    ins.then_inc(self.csem[ename], 1)
            for e2 in ("pe", "act", "dve", "pool"):
                if self.ccnt[e2] > 0:
                    wait(self.csem[e2], e2, self.ccnt[e2])
            for q in DMAQ:
                for i, s in enumerate(self.ring[q]):
                    if ringfinal[q][i] > 0:
                        wait(s, id(s), ringfinal[q][i])
            if ccfinal > 0:
                wait(self.ccsem, id(self.ccsem), ccfinal)

        with nc.Block() as block:
            @block.tensor
            def _(eng):
                emit("pe", eng)

            @block.scalar
            def _(eng):
                emit("act", eng)

            @block.vector
            def _(eng):
                emit("dve", eng)

            @block.gpsimd
            def _(eng):
                emit("pool", eng)

            @block.sync
            def _(eng):
                emit("sp", eng)

        self.ops = {e: [] for e in ENGS}
        self.phase += 1


class Buf:
    def __init__(self, t, name=""):
        self.t = t
        self.r = Res(name)


class Ring:
    def __init__(self, bufs):
        self.bufs = bufs
        self.i = 0

    def next(self):
        b = self.bufs[self.i % len(self.bufs)]
        self.i += 1
        return b


class K:
    def __init__(self, nc, st):
        self.nc = nc
        self.P = Prog(nc, st)
        self.flip = 0
        self.uid = 0

    def sb(self, st, name, shape, dt, side=None):
        self.uid += 1
        return Buf(st.enter_context(self.nc.sbuf_tensor("s%d_%s" % (self.uid, name), shape, dt, side=side)), name)

    def ps(self, st, name, shape=(128, 512), dt=F32):
        self.uid += 1
        return Buf(st.enter_context(self.nc.psum_tensor("p%d_%s" % (self.uid, name), list(shape), dt)), name)

    def dma(self, q, out, in_, reads, writes):
        return self.P.add(q, lambda e: e.dma_start(out=out, in_=in_), reads, writes, dma=True)

    def mm(self, out, lhsT, rhs, start, stop, reads, writes):
        return self.P.add("pe", lambda e: e.matmul(out, lhsT=lhsT, rhs=rhs, start=start, stop=stop), reads, writes)

    def act(self, out, in_, func, reads, writes, **kw):
        return self.P.add("act", lambda e: e.activation(out=out, in_=in_, func=func, **kw), reads, writes)

    def tt(self, eng, out, in0, in1, op, reads, writes):
        return self.P.add(eng, lambda e: e.tensor_tensor(out=out, in0=in0, in1=in1, op=op), reads, writes)

    def ts(self, eng, out, in0, s1, s2, op0, op1, reads, writes):
        if s2 is None:
            return self.P.add(eng, lambda e: e.tensor_scalar(out=out, in0=in0, scalar1=s1, scalar2=None, op0=op0), reads, writes)
        return self.P.add(eng, lambda e: e.tensor_scalar(out=out, in0=in0, scalar1=s1, scalar2=s2, op0=op0, op1=op1), reads, writes)

    def stt(self, eng, out, in0, scalar, in1, op0, op1, reads, writes):
        return self.P.add(eng, lambda e: e.scalar_tensor_tensor(out=out, in0=in0, scalar=scalar, in1=in1, op0=op0, op1=op1), reads, writes)

    def copy(self, eng, out, in_, reads, writes):
        if eng == "act":
            return self.P.add("act", lambda e: e.copy(out=out, in_=in_), reads, writes)
        return self.P.add(eng, lambda e: e.tensor_copy(out=out, in_=in_), reads, writes)

    def evac(self, out, in_, reads, writes):
        self.flip ^= 1
        return self.copy("act" if self.flip else "dve", out, in_, reads, writes)

    def memset(self, eng, ap, val, writes):
        return self.P.add(eng, lambda e: e.memset(ap, val), [], writes)

    def recip(self, out, in_, reads, writes):
        return self.P.add("dve", lambda e: e.reciprocal(out=out, in_=in_), reads, writes)

    def rstd_chain(self, ss, rstd, n, inv_n, post=None):
        self.ts("dve", rstd.t[:, 0:n], ss.t[:, 0:n], inv_n, EPS, ALU.mult, ALU.add, [ss.r], [rstd.r])
        self.act(rstd.t[:, 0:n], rstd.t[:, 0:n], AF.Sqrt, [rstd.r], [rstd.r])
        self.recip(rstd.t[:, 0:n], rstd.t[:, 0:n], [rstd.r], [rstd.r])
        if post is not None:
            self.ts("dve", rstd.t[:, 0:n], rstd.t[:, 0:n], post, None, ALU.mult, None, [rstd.r], [rstd.r])

    def rmsnorm(self, src_ap, src_r, nblk, g, dst_ap, dst_r, junk, ss, rstd):
        srs = (lambda j: [src_r]) if isinstance(src_r, Res) else src_r
        self.memset("dve", ss.t[:, 0:nblk], 0.0, [ss.r])
        for j in range(nblk):
            self.act(junk.t[:, :], src_ap(j), AF.Square, srs(j) + [ss.r], [junk.r, ss.r], accum_out=ss.t[:, j:j + 1])
        self.rstd_chain(ss, rstd, nblk, 1.0 / D)
        for j in range(nblk):
            self.stt("dve", dst_ap(j), src_ap(j), rstd.t[:, j:j + 1], g.t[:, :], ALU.mult, ALU.mult,
                     srs(j) + [rstd.r, g.r], [dst_r])

    def transpose(self, src_ap, src_rs, nblk, nchunk, ident, dst_ap, dst_r, psring):
        for k in range(nchunk):
            ps = psring.next()
            for j in range(nblk):
                self.mm(ps.t[:, j * 128:(j + 1) * 128], src_ap(j, k), ident.t[:, :], True, True,
                        list(src_rs) + [ident.r], [ps.r])
            self.evac(dst_ap(k), ps.t[:, 0:nblk * 128], [ps.r], [dst_r])


def build_program(stop_after=None):
    nc = bass.Bass("TRN2", target_bir_lowering=False)

    def din(name, shape, dt=F32):
        return nc.dram_tensor(name, list(shape), dt, kind="ExternalInput").ap()

    def dscr(name, shape, dt=BF16):
        return nc.dram_tensor(name, list(shape), dt).ap()

    I = {}
    I["x_full"] = din("x_full", [S, D])
    I["x_own"] = din("x_own", [OWN, D])
    I["mem"] = din("mem", [256, D])
    I["gains"] = din("gains", [9, 128, D])
    I["w_in0"] = din("w_in0", [D, 1280])
    I["w_uq"] = din("w_uq", [256, 512])
    I["w_ukv"] = din("w_ukv", [128, 512])
    I["w_out0"] = din("w_out0", [D, D])
    I["g_cq"] = din("g_cq", [128, 2])
    I["g_ckv"] = din("g_ckv", [128, 1])
    I["g_sub"] = din("g_sub", [128, 128])
    I["lamv"] = din("lamv", [128, 4, 64])
    I["w_sb"] = din("w_sb", [D, 1536])
    I["w_sbo"] = din("w_sbo", [D, D])
    I["xa_wq"] = din("xa_wq", [2, D, D])
    I["xa_wkv"] = din("xa_wkv", [2, D, 2 * D])
    I["xa_wo"] = din("xa_wo", [2, D, D])
    I["w1"] = din("w1", [2, D, 4 * D])
    I["w2"] = din("w2", [2, 4 * D, D])
    I["cmat"] = din("cmat", [7, 128, 128])
    I["rope"] = din("rope", [2, 64, S])
    I["btab"] = din("btab", [2, 128, 32])
    I["dtab"] = din("dtab", [7, 128, 128])
    y_out = nc.dram_tensor("y", [OWN, D], F32, kind="ExternalOutput").ap()

    QaT = dscr("QaT", [2, 128, S]); KaT = dscr("KaT", [2, 128, S]); Va = dscr("Va", [2, S, 128])
    QnT = dscr("QnT", [2, 128, S]); QrT = dscr("QrT", [2, 64, S]); KnT = dscr("KnT", [2, 128, S])
    KrT = dscr("KrT", [64, S]); Vb = dscr("Vb", [2, S, 128])
    Osc = [dscr("O0", [S, 512]), dscr("O1", [S, 512])]
    Og = [dscr("O0g", [2, 2 * OWN, 512]), dscr("O1g", [2, 2 * OWN, 512])]
    Hn = dscr("Hn", [OWN, D]); Hg = dscr("Hg", [2, 2 * 1024, D])
    QsT = dscr("QsT", [4, 128, S]); KsT = dscr("KsT", [4, 128, S]); Vs = dscr("Vs", [S, 512])
    dres = Res("dram")

    RG = [[0, 1], [2, 3], [4, 5], [6, 7]]

    with ExitStack() as st:
        k = K(nc, st)
        P = k.P
        h = k.sb(st, "h", [128, 16, D], F32)
        hr = [[Res("h%d_%d" % (j, hf)) for hf in range(2)] for j in range(16)]
        hall = [r_ for row in hr for r_ in row]
        cm = k.sb(st, "cm", [128, 7, 128], BF16)
        ones32 = k.sb(st, "ones32", [128, 128], F32)
        lam = k.sb(st, "lam", [128, 4], F32)
        junk = k.sb(st, "junk", [128, D], F32)
        ss = k.sb(st, "ss", [128, 16], F32)
        rstd = k.sb(st, "rstd", [128, 16], F32)
        ident = Buf(cm.t, "ident")
        ident.r = cm.r

        def IDN():
            return cm.t[:, 0, :]

        class _Id:
            pass

        identb = _Id()
        identb.t = cm.t[:, 0, :]
        identb.r = cm.r

        with ExitStack() as s0:
            lv = k.sb(s0, "lv", [128, 4, 64], F32)
            lt = k.sb(s0, "lt", [128, 2], F32)
            k.dma("pool", cm.t[:, :, :], I["cmat"].rearrange("n p f -> p n f"), [], [cm.r])
            k.memset("dve", ones32.t[:, :], 1.0, [ones32.r])
            k.dma("sp", lv.t[:, :, :], I["lamv"][:, :, :], [], [lv.r])
            k.dma("sp", h.t[:, :, :], I["x_own"].rearrange("(n p) f -> p n f", p=128), [], hall)
            k.memset("dve", lt.t[:, :], 0.0, [lt.r])
            for i in range(2):
                k.P.add("dve", lambda e, i=i: e.tensor_tensor(out=junk.t[:, 0:64], in0=lv.t[:, 2 * i, :], in1=lv.t[:, 2 * i + 1, :], op=ALU.mult),
                        [lv.r], [junk.r])
                k.P.add("dve", lambda e, i=i: e.reduce_sum(out=lt.t[:, i:i + 1], in_=junk.t[:, 0:64], axis=mybir.AxisListType.X),
                        [junk.r], [lt.r])
            k.act(lt.t[:, :], lt.t[:, :], AF.Exp, [lt.r], [lt.r])
            k.stt("dve", lam.t[:, 0:1], lt.t[:, 1:2], -LAMBDA_INIT0, lt.t[:, 0:1], ALU.add, ALU.subtract, [lt.r], [lam.r])
            P.flush()

        def pipeline(items, stages):
            n = len(items)
            ks = len(stages)
            for t in range(n + ks - 1):
                for j, f in enumerate(stages):
                    i = t - j
                    if 0 <= i < n:
                        f(items[i])

        def phase_P0():
            with ExitStack() as s:
                win = k.sb(s, "win", [128, 8, 1280], BF16)
                wuq = k.sb(s, "wuq", [128, 2, 512], BF16)
                wukv = k.sb(s, "wukv", [128, 512], BF16)
                gev = k.sb(s, "gev", [128, D], F32)
                gcq = k.sb(s, "gcq", [128, 2], F32)
                gckv = k.sb(s, "gckv", [128, 1], F32)
                xt = k.sb(s, "xt", [128, 4, D], F32)
                xn = k.sb(s, "xn", [128, 4, D], BF16)
                xnT = [k.sb(s, "xnT%d" % i, [128, 8, 512], BF16) for i in range(2)]
                rope = [k.sb(s, "rope%d" % i, [64, 2, 512], F32) for i in range(3)]
                osb = Ring([k.sb(s, "osb%d" % i, [128, 512], BF16) for i in range(6)])
                vsb = Ring([k.sb(s, "vsb%d" % i, [128, 4, 256], BF16) for i in range(3)])
                sq = [k.sb(s, "sq%d" % i, [128, 512], F32) for i in range(3)]
                rsb = k.sb(s, "rsb", [128, 512], F32)
                rsb2 = k.sb(s, "rsb2", [128, 512], F32)
                cqn = [k.sb(s, "cqn%d" % i, [128, 2, 512], BF16) for i in range(2)]
                ckvn = [k.sb(s, "ckvn%d" % i, [128, 512], BF16) for i in range(2)]
                rr = [k.sb(s, "rr%d" % i, [64, 512], F32) for i in range(4)]
                ss0 = k.sb(s, "ss0", [128, 4], F32)
                rstd0 = k.sb(s, "rstd0", [128, 4], F32)
                psr = Ring([k.ps(s, "ps%d" % i) for i in range(8)])
                win_r = [Res("winA"), Res("winB"), Res("winC")]
                for gi_, (c0_, c1_) in enumerate([(0, 512), (512, 768), (768, 1280)]):
                    k.dma("pool", win.t[:, :, c0_:c1_], I["w_in0"][:, c0_:c1_].rearrange("(k p) c -> p k c", p=128), [], [win_r[gi_]])
                k.dma("pool", wuq.t[:, :, :], I["w_uq"].rearrange("(k p) c -> p k c", p=128), [], [wuq.r])
                k.dma("pool", wukv.t[:, :], I["w_ukv"][:, :], [], [wukv.r])
                k.dma("sp", gev.t[:, :], I["gains"][0], [], [gev.r])
                k.dma("sp", gcq.t[:, :], I["g_cq"][:, :], [], [gcq.r])
                k.dma("sp", gckv.t[:, :], I["g_ckv"][:, :], [], [gckv.r])

                def rope_apply(psA, psB, rp, out_ap, out_r, ra, rb):
                    k.tt("dve", ra.t[:, :], psA.t[0:64, :], rp.t[:, 0, :], ALU.mult, [psA.r, rp.r], [ra.r])
                    k.tt("dve", rb.t[:, :], psB.t[0:64, :], rp.t[:, 1, :], ALU.mult, [psB.r, rp.r], [rb.r])
                    k.tt("dve", out_ap, ra.t[:, :], rb.t[:, :], ALU.add, [ra.r, rb.r], [out_r])

                def sA(t):
                    tok = slice(t * 512, (t + 1) * 512)
                    rp = rope[t % 3]
                    X = xnT[t % 2]
                    k.dma("sp", xt.t[:, :, :], I["x_full"][tok, :].rearrange("(n p) f -> p n f", p=128), [], [xt.r])
                    k.dma("sp", rp.t[:, :, :], I["rope"][:, :, tok].rearrange("n p f -> p n f"), [], [rp.r])
                    k.rmsnorm(lambda j: xt.t[:, j, :], xt.r, 4, gev, lambda j: xn.t[:, j, :], xn.r, junk, ss0, rstd0)
                    k.transpose(lambda j, c: xn.t[:, j, c * 128:(c + 1) * 128], [xn.r], 4, 8, identb, lambda c: X.t[:, c, :], X.r, psr)

                def sB(t):
                    tok = slice(t * 512, (t + 1) * 512)
                    rp = rope[t % 3]
                    X = xnT[t % 2]
                    cq_, ckv_ = cqn[t % 2], ckvn[t % 2]
                    for ci, (dst, hh) in enumerate([(QaT, 0), (QaT, 1), (KaT, 0), (KaT, 1)]):
                        ps = psr.next()
                        for kk in range(8):
                            k.mm(ps.t[:, :], win.t[:, kk, ci * 128:(ci + 1) * 128], X.t[:, kk, :], kk == 0, kk == 7, [win_r[0], X.r], [ps.r])
                        o = osb.next()
                        k.evac(o.t[:, :], ps.t[:, :], [ps.r], [o.r])
                        k.dma("sp", dst[hh][:, tok], o.t[:, :], [o.r], [dres])
                    vo = vsb.next()
                    for j in range(4):
                        ps = psr.next()
                        for kk in range(8):
                            k.mm(ps.t[:, 0:256], X.t[:, kk, j * 128:(j + 1) * 128], win.t[:, kk, 512:768], kk == 0, kk == 7, [win_r[1], X.r], [ps.r])
                        k.evac(vo.t[:, j, :], ps.t[:, 0:256], [ps.r], [vo.r])
                    for hh in range(2):
                        k.dma("sp", Va[hh][tok, :].rearrange("(n p) f -> p n f", p=128), vo.t[:, :, hh * 128:(hh + 1) * 128], [vo.r], [dres])
                    lat = []
                    for ci in range(3):
                        ps = psr.next()
                        c0 = 768 + ci * 128
                        for kk in range(8):
                            k.mm(ps.t[:, :], win.t[:, kk, c0:c0 + 128], X.t[:, kk, :], kk == 0, kk == 7, [win_r[2], X.r], [ps.r])
                        k.act(sq[ci].t[:, :], ps.t[:, :], AF.Square, [ps.r], [sq[ci].r])
                        lat.append(ps)
                    pss = psr.next()
                    for ci in range(2):
                        k.mm(pss.t[:, :], ones32.t[:, :], sq[ci].t[:, :], ci == 0, ci == 1, [ones32.r, sq[ci].r], [pss.r])
                    pss2 = psr.next()
                    k.mm(pss2.t[:, :], ones32.t[:, :], sq[2].t[:, :], True, True, [ones32.r, sq[2].r], [pss2.r])
                    psA = psr.next()
                    psB = psr.next()
                    for kk in range(8):
                        k.mm(psA.t[0:64, :], win.t[:, kk, 1152:1216], X.t[:, kk, :], kk == 0, kk == 7, [win_r[2], X.r], [psA.r])
                    for kk in range(8):
                        k.mm(psB.t[0:64, :], win.t[:, kk, 1216:1280], X.t[:, kk, :], kk == 0, kk == 7, [win_r[2], X.r], [psB.r])
                    k.ts("dve", rsb.t[:, :], pss.t[:, :], 1.0 / 256, EPS, ALU.mult, ALU.add, [pss.r], [rsb.r])
                    k.ts("dve", rsb2.t[:, :], pss2.t[:, :], 1.0 / 128, EPS, ALU.mult, ALU.add, [pss2.r], [rsb2.r])
                    k.act(rsb.t[:, :], rsb.t[:, :], AF.Sqrt, [rsb.r], [rsb.r])
                    k.act(rsb2.t[:, :], rsb2.t[:, :], AF.Sqrt, [rsb2.r], [rsb2.r])
                    k.recip(rsb.t[:, :], rsb.t[:, :], [rsb.r], [rsb.r])
                    k.recip(rsb2.t[:, :], rsb2.t[:, :], [rsb2.r], [rsb2.r])
                    for ci in range(2):
                        k.stt("dve", cq_.t[:, ci, :], lat[ci].t[:, :], gcq.t[:, ci:ci + 1], rsb.t[:, :], ALU.mult, ALU.mult,
                              [lat[ci].r, gcq.r, rsb.r], [cq_.r])
                    k.stt("dve", ckv_.t[:, :], lat[2].t[:, :], gckv.t[:, 0:1], rsb2.t[:, :], ALU.mult, ALU.mult,
                          [lat[2].r, gckv.r, rsb2.r], [ckv_.r])
                    o = osb.next()
                    rope_apply(psA, psB, rp, o.t[0:64, :], o.r, rr[0], rr[1])
                    k.dma("sp", KrT[:, tok], o.t[0:64, :], [o.r], [dres])

                def sC(t):
                    tok = slice(t * 512, (t + 1) * 512)
                    rp = rope[t % 3]
                    cq_, ckv_ = cqn[t % 2], ckvn[t % 2]
                    for hh in range(2):
                        ps = psr.next()
                        for ci in range(2):
                            k.mm(ps.t[:, :], wuq.t[:, ci, hh * 128:(hh + 1) * 128], cq_.t[:, ci, :], ci == 0, ci == 1, [wuq.r, cq_.r], [ps.r])
                        o = osb.next()
                        k.evac(o.t[:, :], ps.t[:, :], [ps.r], [o.r])
                        k.dma("sp", QnT[hh][:, tok], o.t[:, :], [o.r], [dres])
                        psA = psr.next()
                        psB = psr.next()
                        c0 = 256 + hh * 128
                        for ci in range(2):
                            k.mm(psA.t[0:64, :], wuq.t[:, ci, c0:c0 + 64], cq_.t[:, ci, :], ci == 0, ci == 1, [wuq.r, cq_.r], [psA.r])
                        for ci in range(2):
                            k.mm(psB.t[0:64, :], wuq.t[:, ci, c0 + 64:c0 + 128], cq_.t[:, ci, :], ci == 0, ci == 1, [wuq.r, cq_.r], [psB.r])
                        o = osb.next()
                        rope_apply(psA, psB, rp, o.t[0:64, :], o.r, rr[2], rr[3])
                        k.dma("sp", QrT[hh][:, tok], o.t[0:64, :], [o.r], [dres])
                    for hh in range(2):
                        ps = psr.next()
                        k.mm(ps.t[:, :], wukv.t[:, hh * 128:(hh + 1) * 128], ckv_.t[:, :], True, True, [wukv.r, ckv_.r], [ps.r])
                        o = osb.next()
                        k.evac(o.t[:, :], ps.t[:, :], [ps.r], [o.r])
                        k.dma("sp", KnT[hh][:, tok], o.t[:, :], [o.r], [dres])
                    vo = vsb.next()
                    for j in range(4):
                        ps = psr.next()
                        k.mm(ps.t[:, 0:256], ckv_.t[:, j * 128:(j + 1) * 128], wukv.t[:, 256:512], True, True, [wukv.r, ckv_.r], [ps.r])
                        k.evac(vo.t[:, j, :], ps.t[:, 0:256], [ps.r], [vo.r])
                    for hh in range(2):
                        k.dma("sp", Vb[hh][tok, :].rearrange("(n p) f -> p n f", p=128), vo.t[:, :, hh * 128:(hh + 1) * 128], [vo.r], [dres])

                pipeline(list(range(8)), [sA, sB, sC])
                P.flush()

        def phase_A0(pre=None):
            with ExitStack() as s:
                if pre is not None:
                    pre()
                kt = [k.sb(s, "kt%d" % i, [128, S], BF16) for i in range(2)]
                qt = [k.sb(s, "qt%d" % i, [128, S], BF16) for i in range(2)]
                kr = k.sb(s, "kr", [64, S], BF16)
                qr = [k.sb(s, "qr%d" % i, [64, S], BF16) for i in range(2)]
                vt = [k.sb(s, "vt%d" % i, [128, NB, 129], BF16) for i in range(2)]
                btab = k.sb(s, "btab", [128, 2, 32], F32)
                dtab = k.sb(s, "dtab", [128, 7, 128], F32)
                gsub = k.sb(s, "gsub", [128, 128], F32)
                tmpd = Ring([k.sb(s, "tmpd%d" % i, [128, 128], F32) for i in range(2)])
                pring = Ring([k.sb(s, "pT%d" % i, [128, 512], BF16) for i in range(3)])
                o1n = k.sb(s, "o1n", [128, 4, 128], F32)
                o2n = k.sb(s, "o2n", [128, 4, 128], F32)
                od = k.sb(s, "od", [128, 4, 128], F32)
                obf = Ring([k.sb(s, "obf%d" % i, [128, 4, 128], BF16) for i in range(2)])
                rs = k.sb(s, "rs", [128, 4], F32)
                ss2 = k.sb(s, "ss2", [128, 4], F32)
                rstd2 = k.sb(s, "rstd2", [128, 4], F32)
                sring = Ring([k.ps(s, "sps%d" % i) for i in range(3)])
                oring = Ring([k.ps(s, "ops%d" % i) for i in range(4)])
                k.dma("sp", btab.t[:, :, :], I["btab"].rearrange("n p f -> p n f"), [], [btab.r])
                k.dma("sp", dtab.t[:, :, :], I["dtab"].rearrange("n p f -> p n f"), [], [dtab.r])
                k.dma("sp", gsub.t[:, :], I["g_sub"][:, :], [], [gsub.r])
                k.ts("dve", gsub.t[:, :], gsub.t[:, :], 1.0 - LAMBDA_INIT0, None, ALU.mult, None, [gsub.r], [gsub.r])
                for i in range(2):
                    k.memset("pool", vt[i].t[:, :, :], 1.0, [vt[i].r])
                k.dma("sp", kr.t[:, :], KrT[:, :], [dres], [kr.r])

                def load_diff(hh):
                    k.dma("sp", kt[hh].t[:, :], KaT[hh][:, :], [dres], [kt[hh].r])
                    k.dma("sp", qt[hh].t[:, :], QaT[hh][:, :], [dres], [qt[hh].r])
                    k.dma("sp", vt[hh].t[:, :, 0:128], Va[hh].rearrange("(n p) f -> p n f", p=128), [dres], [vt[hh].r])

                def load_mla(hh):
                    k.dma("sp", kt[hh].t[:, :], KnT[hh][:, :], [dres], [kt[hh].r])
                    k.dma("sp", qt[hh].t[:, :], QnT[hh][:, :], [dres], [qt[hh].r])
                    k.dma("sp", qr[hh].t[:, :], QrT[hh][:, :], [dres], [qr[hh].r])
                    k.dma("sp", vt[hh].t[:, :, 0:128], Vb[hh].rearrange("(n p) f -> p n f", p=128), [dres], [vt[hh].r])

                items = []
                for kind in ("diff", "mla"):
                    for hh in range(2):
                        first_of_head = len(items)
                        for qg in range(8):
                            for m in (range(2) if kind == "diff" else range(1)):
                                ob = [oring.next(), oring.next()]
                                for kb in range(0, 4 * qg + 4):
                                    items.append(dict(kind=kind, hh=hh, qg=qg, m=m, kb=kb, ob=ob, last=(kb == 4 * qg + 3), pre=None))
                        nxt = None
                        if kind == "diff" and hh == 1:
                            nxt = (lambda: load_mla(0))
                        if kind == "mla" and hh == 0:
                            nxt = (lambda: load_mla(1))
                        if nxt is not None:
                            items[first_of_head + 4]["pre"] = nxt
                load_diff(0)
                load_diff(1)

                def geom(it):
                    qg, kb = it["qg"], it["kb"]
                    n0 = max(0, kb - 4 * qg)
                    return n0, (4 - n0) * 128, qg * 512 + n0 * 128

                def s0(it):
                    if it["pre"] is not None:
                        it["pre"]()
                    n0, N, c0 = geom(it)
                    hh, kb = it["hh"], it["kb"]
                    sp = sring.next()
                    it["sp"] = sp
                    kb_, qb_ = kt[hh], qt[hh]
                    if it["kind"] == "diff":
                        pb = it["m"] * 64
                        k.mm(sp.t[:, 0:N], kb_.t[pb:pb + 64, kb * 128:(kb + 1) * 128], qb_.t[pb:pb + 64, c0:c0 + N], True, True,
                             [kb_.r, qb_.r], [sp.r])
                    else:
                        k.mm(sp.t[:, 0:N], kb_.t[:, kb * 128:(kb + 1) * 128], qb_.t[:, c0:c0 + N], True, False, [kb_.r, qb_.r], [sp.r])
                        k.mm(sp.t[:, 0:N], kr.t[:, kb * 128:(kb + 1) * 128], qr[hh].t[:, c0:c0 + N], False, True, [kr.r, qr[hh].r], [sp.r])

                def s1(it):
                    n0, N, c0 = geom(it)
                    hh, kb, qg = it["hh"], it["kb"], it["qg"]
                    sp = it["sp"]
                    pT = pring.next()
                    it["pT"] = pT
                    diag = kb >= 4 * qg
                    if it["kind"] == "mla":
                        scale, dti, mode = MLA_SCALE, 6, "none"
                    elif hh == 0:
                        scale, dti, mode = DIFF_SCALE, n0 % 2, "pair"
                    else:
                        scale, dti, mode = DIFF_SCALE, 2 + n0, "group"
                    if diag:
                        td = tmpd.next()
                        k.stt("dve", td.t[:, :], sp.t[:, 0:128], scale, dtab.t[:, dti, :], ALU.mult, ALU.add, [sp.r, dtab.r], [td.r])
                        k.act(pT.t[:, 0:128], td.t[:, :], AF.Exp, [td.r], [pT.r])
                    lo = 128 if diag else 0
                    if mode == "pair":
                        for p in range(2):
                            ns = [n for n in range(n0 + (1 if diag else 0), 4) if n // 2 == p]
                            if not ns:
                                continue
                            cs = (ns[0] - n0) * 128
                            ce = (ns[-1] - n0 + 1) * 128
                            g = 4 * qg + 2 * p - kb + 1
                            k.act(pT.t[:, cs:ce], sp.t[:, cs:ce], AF.Exp, [sp.r, btab.r], [pT.r],
                                  scale=scale, bias=btab.t[:, 0, g:g + 1])
                    elif N > lo:
                        if mode == "group":
                            g = 4 * qg - kb + 3
                            k.act(pT.t[:, lo:N], sp.t[:, lo:N], AF.Exp, [sp.r, btab.r], [pT.r], scale=scale, bias=btab.t[:, 1, g:g + 1])
                        else:
                            k.act(pT.t[:, lo:N], sp.t[:, lo:N], AF.Exp, [sp.r], [pT.r], scale=scale)

                def normalize(ob, dst_ap, dst_r):
                    for n in range(4):
                        o = ob[n // 2]
                        c = (n % 2) * 129
                        k.recip(rs.t[:, n:n + 1], o.t[:, c + 128:c + 129], [o.r], [rs.r])
                    for n in range(4):
                        o = ob[n // 2]
                        c = (n % 2) * 129
                        k.ts("dve", dst_ap(n), o.t[:, c:c + 128], rs.t[:, n:n + 1], None, ALU.mult, None, [o.r, rs.r], [dst_r])

                def s2(it):
                    n0, N, c0 = geom(it)
                    hh, kb, qg, ob, pT = it["hh"], it["kb"], it["qg"], it["ob"], it["pT"]
                    V = vt[hh]
                    for n in range(n0, 4):
                        cs = (n - n0) * 128
                        o = ob[n // 2]
                        k.mm(o.t[:, (n % 2) * 129:(n % 2) * 129 + 129], pT.t[:, cs:cs + 128], V.t[:, kb, :],
                             kb == 0 and n % 2 == 0, kb == 4 * qg + n, [pT.r, V.r], [o.r])
                    if not it["last"]:
                        return
                    osl = Osc[0].rearrange("(n p) f -> p n f", p=128)
                    if it["kind"] == "mla":
                        ob_ = obf.next()
                        normalize(ob, lambda n: ob_.t[:, n, :], ob_.r)
                        k.dma("sp", osl[:, 4 * qg:4 * qg + 4, 256 + hh * 128:256 + (hh + 1) * 128], ob_.t[:, :, :], [ob_.r], [dres])
                        return
                    tgt = o1n if it["m"] == 0 else o2n
                    normalize(ob, lambda n: tgt.t[:, n, :], tgt.r)
                    if it["m"] == 0:
                        return
                    k.stt("dve", od.t[:, :, :], o2n.t[:, :, :], lam.t[:, 0:1], o1n.t[:, :, :], ALU.mult, ALU.add,
                          [o2n.r, o1n.r, lam.r], [od.r])
                    k.memset("dve", ss2.t[:, 0:4], 0.0, [ss2.r])
                    for n in range(4):
                        k.act(junk.t[:, 0:128], od.t[:, n, :], AF.Square, [od.r, ss2.r], [junk.r, ss2.r], accum_out=ss2.t[:, n:n + 1])
                    k.rstd_chain(ss2, rstd2, 4, 1.0 / 128)
                    ob_ = obf.next()
                    for n in range(4):
                        k.stt("dve", ob_.t[:, n, :], od.t[:, n, :], rstd2.t[:, n:n + 1], gsub.t[:, :], ALU.mult, ALU.mult,
                              [od.r, rstd2.r, gsub.r], [ob_.r])
                    k.dma("sp", osl[:, 4 * qg:4 * qg + 4, hh * 128:(hh + 1) * 128], ob_.t[:, :, :], [ob_.r], [dres])

                pipeline(items, [s0, s1, s2])
                P.flush()

        def phase_X(pairs, pre=None, flush=True):
            if pre is not None:
                pre()
            for src, dst in pairs:
                P.add("pool", lambda e, src=src, dst=dst: e.collective_compute("AllGather", ALU.bypass, replica_groups=RG, ins=[src], outs=[dst]),
                      [dres], [dres], cc=True)
            if flush:
                P.flush()

        def phase_M(layer, kxT, vx, wkv):
            with ExitStack() as s:
                gm = k.sb(s, "gm", [128, D], F32)
                mt = k.sb(s, "mt", [128, 2, D], F32)
                mn = k.sb(s, "mn", [128, 2, D], BF16)
                mT = k.sb(s, "mT", [128, 8, 256], BF16)
                psr = Ring([k.ps(s, "psm%d" % i) for i in range(6)])
                k.dma("sp", gm.t[:, :], I["gains"][4 + layer], [], [gm.r])
                k.dma("sp", mt.t[:, :, :], I["mem"].rearrange("(n p) f -> p n f", p=128), [], [mt.r])
                k.rmsnorm(lambda j: mt.t[:, j, :], mt.r, 2, gm, lambda j: mn.t[:, j, :], mn.r, junk, ss, rstd)
                k.transpose(lambda j, c: mn.t[:, j, c * 128:(c + 1) * 128], [mn.r], 2, 8, identb, lambda c: mT.t[:, c, :], mT.r, psr)
                for cc in range(8):
                    ps = psr.next()
                    for kk in range(8):
                        k.mm(ps.t[:, 0:256], wkv.t[:, kk, cc * 128:(cc + 1) * 128], mT.t[:, kk, :], kk == 0, kk == 7, [wkv.r, mT.r], [ps.r])
                    k.evac(kxT.t[:, cc, :], ps.t[:, 0:256], [ps.r], [kxT.r])
                k.memset("pool", vx.t[:, :, :, :], 1.0, [vx.r])
                for mb in range(2):
                    for half in range(2):
                        ps = psr.next()
                        for kk in range(8):
                            k.mm(ps.t[:, :], mT.t[:, kk, mb * 128:(mb + 1) * 128], wkv.t[:, kk, D + half * 512:D + (half + 1) * 512],
                                 kk == 0, kk == 7, [wkv.r, mT.r], [ps.r])
                        for hx in range(2):
                            k.evac(vx.t[:, mb, half * 2 + hx, 0:256], ps.t[:, hx * 256:(hx + 1) * 256], [ps.r], [vx.r])
                P.flush()

        def phase_Ra(layer, Ogath, chunk_map, wkv, free_wkv, wout, wq, wo):
            with ExitStack() as s:
                kxT = k.sb(s, "kxT", [128, 8, 256], BF16)
                vx = k.sb(s, "vx", [128, 2, 4, 257], BF16)
                phase_M(layer, kxT, vx, wkv)
                free_wkv()
                gx = k.sb(s, "gx", [128, D], F32)
                ol = [k.sb(s, "ol%d" % hf, [128, 4, 512], BF16) for hf in range(2)]
                oT = k.sb(s, "oT", [128, 8, 512], BF16)
                hn = k.sb(s, "hn", [128, 4, D], BF16)
                hnT = [k.sb(s, "hnT%d" % i, [128, 8, 512], BF16) for i in range(2)]
                qx = Ring([k.sb(s, "qx%d" % i, [128, 2, 512], BF16) for i in range(2)])
                pring = Ring([k.sb(s, "pTx%d" % i, [128, 512], BF16) for i in range(4)])
                oxn = [k.sb(s, "oxn%d" % i, [128, 4, D], BF16) for i in range(2)]
                oxT = k.sb(s, "oxT", [128, 8, 512], BF16)
                rs = k.sb(s, "rsx", [128, 4], F32)
                ssx = k.sb(s, "ssx", [128, 4], F32)
                rstdx = k.sb(s, "rstdx", [128, 4], F32)
                psr = Ring([k.ps(s, "psa%d" % i) for i in range(5)])
                oring = Ring([k.ps(s, "psox%d" % i) for i in range(3)])
                k.dma("sp", gx.t[:, :], I["gains"][2 + layer], [], [gx.r])

                def proj_add(t, srcT, w):
                    for j in range(4):
                        for half in range(2):
                            ps = psr.next()
                            for kk in range(8):
                                k.mm(ps.t[:, :], srcT.t[:, kk, j * 128:(j + 1) * 128], w.t[:, kk, half * 512:(half + 1) * 512],
                                     kk == 0, kk == 7, [srcT.r, w.r], [ps.r])
                            hap = h.t[:, t * 4 + j, half * 512:(half + 1) * 512]
                            hres = hr[t * 4 + j][half]
                            k.tt("dve", hap, ps.t[:, :], hap, ALU.add, [ps.r, hres], [hres])

                def sA(t):
                    for r in range(2):
                        for hf in range(2):
                            row0 = r * OWN + t * 512
                            k.dma("sp", ol[hf].t[:, :, :], Ogath[hf][row0:row0 + 512, :].rearrange("(n p) f -> p n f", p=128),
                                  [dres], [ol[hf].r])
                        for c4 in range(4):
                            ps = psr.next()
                            for j in range(4):
                                for hf in range(2):
                                    k.mm(ps.t[:, j * 128:(j + 1) * 128], ol[hf].t[:, j, c4 * 128:(c4 + 1) * 128], cm.t[:, 1 + hf, :],
                                         hf == 0, hf == 1, [ol[hf].r, cm.r], [ps.r])
                            k.evac(oT.t[:, chunk_map(r, c4), :], ps.t[:, :], [ps.r], [oT.r])
                    proj_add(t, oT, wout)
                    X = hnT[t % 2]
                    k.rmsnorm(lambda j: h.t[:, t * 4 + j, :], lambda j: list(hr[t * 4 + j]), 4, gx, lambda j: hn.t[:, j, :], hn.r, junk, ssx, rstdx)
                    k.transpose(lambda j, c: hn.t[:, j, c * 128:(c + 1) * 128], [hn.r], 4, 8, identb, lambda c: X.t[:, c, :], X.r, psr)

                def sB(t):
                    X = hnT[t % 2]
                    ox = oxn[t % 2]
                    for hx in range(4):
                        q = qx.next()
                        for c2 in range(2):
                            ps = psr.next()
                            c0 = hx * 256 + c2 * 128
                            for kk in range(8):
                                k.mm(ps.t[:, :], wq.t[:, kk, c0:c0 + 128], X.t[:, kk, :], kk == 0, kk == 7, [wq.r, X.r], [ps.r])
                            k.evac(q.t[:, c2, :], ps.t[:, :], [ps.r], [q.r])
                        ob = [oring.next(), oring.next()]
                        pts = []
                        for mb in range(2):
                            ps = psr.next()
                            for c2 in range(2):
                                k.mm(ps.t[:, :], kxT.t[:, hx * 2 + c2, mb * 128:(mb + 1) * 128], q.t[:, c2, :], c2 == 0, c2 == 1,
                                     [kxT.r, q.r], [ps.r])
                            pT = pring.next()
                            k.act(pT.t[:, :], ps.t[:, :], AF.Exp, [ps.r], [pT.r], scale=XA_SCALE)
                            pts.append(pT)
                        for j in range(4):
                            o = ob[j // 2]
                            for mb in range(2):
                                k.mm(o.t[:, (j % 2) * 256:(j % 2) * 256 + 256], pts[mb].t[:, j * 128:(j + 1) * 128], vx.t[:, mb, hx, 0:256],
                                     mb == 0, mb == 1, [pts[mb].r, vx.r], [o.r])
                        sb_ = oring.next()
                        for j in range(4):
                            for mb in range(2):
                                k.mm(sb_.t[:, j:j + 1], pts[mb].t[:, j * 128:(j + 1) * 128], vx.t[:, mb, hx, 256:257],
                                     mb == 0, mb == 1, [pts[mb].r, vx.r], [sb_.r])
                        k.recip(rs.t[:, 0:4], sb_.t[:, 0:4], [sb_.r], [rs.r])
                        for j in range(4):
                            o = ob[j // 2]
                            k.ts("dve", ox.t[:, j, hx * 256:(hx + 1) * 256], o.t[:, (j % 2) * 256:(j % 2) * 256 + 256], rs.t[:, j:j + 1], None,
                                 ALU.mult, None, [o.r, rs.r], [ox.r])

                def sC(t):
                    ox = oxn[t % 2]
                    k.transpose(lambda j, c: ox.t[:, j, c * 128:(c + 1) * 128], [ox.r], 4, 8, identb, lambda c: oxT.t[:, c, :], oxT.r, psr)
                    proj_add(t, oxT, wo)

                pipeline(list(range(4)), [sA, sB, sC])
                P.flush()

        def phase_Rb(layer):
            with ExitStack() as s:
                gm = k.sb(s, "gmlp", [128, D], F32)
                hn = k.sb(s, "hnm", [128, 4, D], BF16)
                hnT = k.sb(s, "hnTm", [128, 8, OWN], BF16)
                w1g = [k.sb(s, "w1g%d" % i, [128, 8, 512], BF16) for i in range(2)]
                w2g = [k.sb(s, "w2g%d" % i, [128, 4, D], BF16) for i in range(2)]
                uT = [k.sb(s, "uT%d" % i, [128, 4, OWN], BF16) for i in range(2)]
                rl = Ring([k.sb(s, "rl%d" % i, [128, 512], F32) for i in range(3)])
                psr = Ring([k.ps(s, "psb%d" % i) for i in range(4)])
                psy = Ring([k.ps(s, "psy%d" % i) for i in range(4)])
                hnT_r = [Res("hnT%d" % i) for i in range(4)]
                k.dma("sp", gm.t[:, :], I["gains"][6 + layer], [], [gm.r])
                for t in range(4):
                    k.rmsnorm(lambda j: h.t[:, t * 4 + j, :], lambda j: list(hr[t * 4 + j]), 4, gm, lambda j: hn.t[:, j, :], hn.r, junk, ss, rstd)
                    k.transpose(lambda j, c: hn.t[:, j, c * 128:(c + 1) * 128], [hn.r], 4, 8, identb,
                                lambda c: hnT.t[:, c, t * 512:(t + 1) * 512], hnT_r[t], psr)
                for fg in range(8):
                    w1 = w1g[fg % 2]
                    w2 = w2g[fg % 2]
                    u = uT[fg % 2]
                    k.dma("pool", w1.t[:, :, :], I["w1"][layer][:, fg * 512:(fg + 1) * 512].rearrange("(k p) c -> p k c", p=128), [], [w1.r])
                    k.dma("pool", w2.t[:, :, :], I["w2"][layer][fg * 512:(fg + 1) * 512, :].rearrange("(k p) c -> p k c", p=128), [], [w2.r])
                    for t in range(4):
                        for fc in range(4):
                            ps = psr.next()
                            for kk in range(8):
                                k.mm(ps.t[:, :], w1.t[:, kk, fc * 128:(fc + 1) * 128], hnT.t[:, kk, t * 512:(t + 1) * 512], kk == 0, kk == 7,
                                     [w1.r, hnT_r[t]], [ps.r])
                            r_ = rl.next()
                            k.act(r_.t[:, :], ps.t[:, :], AF.Relu, [ps.r], [r_.r])
                            k.tt("pool", u.t[:, fc, t * 512:(t + 1) * 512], r_.t[:, :], r_.t[:, :], ALU.mult, [r_.r], [u.r])
                    for j in range(16):
                        for half in range(2):
                            ps = psy.next()
                            for fc in range(4):
                                k.mm(ps.t[:, :], u.t[:, fc, j * 128:(j + 1) * 128], w2.t[:, fc, half * 512:(half + 1) * 512], fc == 0, fc == 3,
                                     [u.r, w2.r], [ps.r])
                            hap = h.t[:, j, half * 512:(half + 1) * 512]
                            k.tt("dve", hap, ps.t[:, :], hap, ALU.add, [ps.r, hr[j][half]], [hr[j][half]])
                P.flush()

        def phase_E1():
            with ExitStack() as s:
                g = k.sb(s, "god", [128, D], F32)
                hb = k.sb(s, "hb", [128, 16, D], BF16)
                k.dma("sp", g.t[:, :], I["gains"][1], [], [g.r])
                for t in range(4):
                    k.rmsnorm(lambda j: h.t[:, t * 4 + j, :], lambda j: list(hr[t * 4 + j]), 4, g, lambda j: hb.t[:, t * 4 + j, :], hb.r, junk, ss, rstd)
                k.dma("sp", Hn.rearrange("(n p) f -> p n f", p=128), hb.t[:, :, :], [hb.r], [dres])
                P.flush()

        def phase_P1():
            with ExitStack() as s:
                wsb = k.sb(s, "wsb", [128, 8, 1536], BF16)
                xn = [k.sb(s, "xn1%d" % i, [128, 4, D], BF16) for i in range(2)]
                xnT = [k.sb(s, "xnT1%d" % i, [128, 8, 512], BF16) for i in range(2)]
                osb = Ring([k.sb(s, "osb1%d" % i, [128, 512], BF16) for i in range(4)])
                vsb = Ring([k.sb(s, "vsb1%d" % i, [128, 4, 512], BF16) for i in range(2)])
                psr = Ring([k.ps(s, "psp%d" % i) for i in range(8)])
                wsb_r = [Res("wsbQ"), Res("wsbK"), Res("wsbV")]
                for gi_ in range(3):
                    k.dma("pool", wsb.t[:, :, gi_ * 512:(gi_ + 1) * 512], I["w_sb"][:, gi_ * 512:(gi_ + 1) * 512].rearrange("(k p) c -> p k c", p=128),
                          [], [wsb_r[gi_]])

                def sA(t):
                    xb = xn[t % 2]
                    X = xnT[t % 2]
                    hrow = (t // 4) * 1024 + ((t % 4) % 2) * 512
                    k.dma("sp", xb.t[:, :, :], Hg[(t % 4) // 2][hrow:hrow + 512, :].rearrange("(n p) f -> p n f", p=128), [dres], [xb.r])
                    k.transpose(lambda j, c: xb.t[:, j, c * 128:(c + 1) * 128], [xb.r], 4, 8, identb, lambda c: X.t[:, c, :], X.r, psr)

                def sB(t):
                    tok = slice(t * 512, (t + 1) * 512)
                    X = xnT[t % 2]
                    for ci in range(8):
                        ps = psr.next()
                        for kk in range(8):
                            k.mm(ps.t[:, :], wsb.t[:, kk, ci * 128:(ci + 1) * 128], X.t[:, kk, :], kk == 0, kk == 7, [wsb_r[ci // 4], X.r], [ps.r])
                        o = osb.next()
                        if ci < 4:
                            k.act(o.t[:, :], ps.t[:, :], AF.Copy, [ps.r], [o.r], scale=0.125)
                            k.dma("sp", QsT[ci][:, tok], o.t[:, :], [o.r], [dres])
                        else:
                            k.evac(o.t[:, :], ps.t[:, :], [ps.r], [o.r])
                            k.dma("sp", KsT[ci - 4][:, tok], o.t[:, :], [o.r], [dres])
                    vo = vsb.next()
                    for j in range(4):
                        ps = psr.next()
                        for kk in range(8):
                            k.mm(ps.t[:, :], X.t[:, kk, j * 128:(j + 1) * 128], wsb.t[:, kk, 1024:1536], kk == 0, kk == 7, [wsb_r[2], X.r], [ps.r])
                        k.evac(vo.t[:, j, :], ps.t[:, :], [ps.r], [vo.r])
                    k.dma("sp", Vs[tok, :].rearrange("(n p) f -> p n f", p=128), vo.t[:, :, :], [vo.r], [dres])

                pipeline(list(range(8)), [sA, sB])
                P.flush()

        def phase_A1(pre=None):
            with ExitStack() as s:
                if pre is not None:
                    pre()
                kt = [k.sb(s, "skt%d" % i, [128, S], BF16) for i in range(2)]
                qt = [k.sb(s, "sqt%d" % i, [128, S], BF16) for i in range(2)]
                vt = k.sb(s, "svt", [128, NB, 512], BF16)
                esb = Ring([k.sb(s, "esb%d" % i, [128, 512], F32) for i in range(2)])
                spb = Ring([k.sb(s, "spb%d" % i, [128, 512], BF16) for i in range(3)])
                Rring = Ring([k.sb(s, "Rb%d" % i, [128, 512], BF16) for i in range(3)])
                pring = Ring([k.sb(s, "spT%d" % i, [128, 512], BF16) for i in range(3)])
                obf = Ring([k.sb(s, "sobf%d" % i, [128, 4, 64], BF16) for i in range(2)])
                zring = Ring([k.ps(s, "zps%d" % i) for i in range(3)])
                aring = Ring([k.ps(s, "aps%d" % i) for i in range(3)])
                oring = Ring([k.ps(s, "sops%d" % i) for i in range(2)])
                trineg = cm.t[:, 3, :]
                negones = cm.t[:, 4, :]
                sbmask = cm.t[:, 5, :]
                k.dma("sp", vt.t[:, :, :], Vs.rearrange("(n p) f -> p n f", p=128), [dres], [vt.r])

                def load_pair(hp):
                    k.dma("sp", kt[hp % 2].t[:, :], KsT[hp][:, :], [dres], [kt[hp % 2].r])
                    k.dma("sp", qt[hp % 2].t[:, :], QsT[hp][:, :], [dres], [qt[hp % 2].r])

                items = []
                for hp in range(4):
                    first_of_pair = len(items)
                    for hl in range(2):
                        for qg in range(8):
                            o = oring.next()
                            prev = None
                            for ui, kb in enumerate(range(4 * qg + 3, -1, -1)):
                                it = dict(hp=hp, hl=hl, qg=qg, kb=kb, ui=ui, o=o, pre=None)
                                it["Rin"] = prev["Rout"] if prev is not None else None
                                it["Rout"] = Rring.next() if kb > 0 else None
                                items.append(it)
                                prev = it
                    if hp + 1 < 4:
                        items[first_of_pair + 6]["pre"] = (lambda hp=hp: load_pair(hp + 1))
                load_pair(0)

                def geom(it):
                    qg, kb = it["qg"], it["kb"]
                    n0 = max(0, kb - 4 * qg)
                    return n0, (4 - n0) * 128, qg * 512 + n0 * 128

                def qk(it, dst, start, stop):
                    n0, N, c0 = geom(it)
                    kb = it["kb"]
                    pb = it["hl"] * 64
                    kb_, qb_ = kt[it["hp"] % 2], qt[it["hp"] % 2]
                    k.mm(dst.t[:, 0:N], kb_.t[pb:pb + 64, kb * 128:(kb + 1) * 128], qb_.t[pb:pb + 64, c0:c0 + N], start, stop,
                         [kb_.r, qb_.r], [dst.r])

                def s0(it):
                    if it["pre"] is not None:
                        it["pre"]()
                    it["zp"] = zring.next()
                    qk(it, it["zp"], True, True)

                def s1(it):
                    n0, N, c0 = geom(it)
                    off = n0 * 128
                    diag = it["kb"] >= 4 * it["qg"]
                    zp = it["zp"]
                    e_ = esb.next()
                    k.act(e_.t[:, 0:N], zp.t[:, 0:N], AF.Exp, [zp.r], [e_.r])
                    sp_ = spb.next()
                    it["sp"] = sp_
                    k.act(sp_.t[:, 0:N], e_.t[:, 0:N], AF.Ln, [e_.r], [sp_.r], bias=1.0)
                    if diag:
                        k.tt("pool", sp_.t[:, 0:128], sp_.t[:, 0:128], sbmask, ALU.mult, [sp_.r, cm.r], [sp_.r])
                    Rn, Rin = it["Rout"], it["Rin"]
                    if Rn is not None:
                        if Rin is None:
                            k.memset("pool", Rn.t[:, :], 0.0, [Rn.r])
                            k.copy("pool", Rn.t[:, off:off + N], sp_.t[:, 0:N], [sp_.r], [Rn.r])
                        else:
                            if off > 0:
                                k.memset("pool", Rn.t[:, 0:off], 0.0, [Rn.r])
                            k.tt("dve", Rn.t[:, off:off + N], Rin.t[:, off:off + N], sp_.t[:, 0:N], ALU.add, [Rin.r, sp_.r], [Rn.r])

                def s2(it):
                    n0, N, c0 = geom(it)
                    off = n0 * 128
                    ap_ = aring.next()
                    it["ap"] = ap_
                    sp_, Rin = it["sp"], it["Rin"]
                    qk(it, ap_, True, False)
                    k.mm(ap_.t[:, 0:N], trineg, sp_.t[:, 0:N], False, Rin is None, [cm.r, sp_.r], [ap_.r])
                    if Rin is not None:
                        k.mm(ap_.t[:, 0:N], negones, Rin.t[:, off:off + N], False, True, [cm.r, Rin.r], [ap_.r])

                def s3(it):
                    n0, N, c0 = geom(it)
                    diag = it["kb"] >= 4 * it["qg"]
                    ap_ = it["ap"]
                    pT = pring.next()
                    it["pT"] = pT
                    k.act(pT.t[:, 0:N], ap_.t[:, 0:N], AF.Exp, [ap_.r], [pT.r])
                    if diag:
                        k.tt("pool", pT.t[:, 0:128], pT.t[:, 0:128], sbmask, ALU.mult, [pT.r, cm.r], [pT.r])

                def s4(it):
                    n0, N, c0 = geom(it)
                    kb, qg, o, pT = it["kb"], it["qg"], it["o"], it["pT"]
                    hd = it["hp"] * 2 + it["hl"]
                    for n in range(n0, 4):
                        cs = (n - n0) * 128
                        k.mm(o.t[:, n * 64:(n + 1) * 64], pT.t[:, cs:cs + 128], vt.t[:, kb, hd * 64:(hd + 1) * 64],
                             it["ui"] == 0, kb == 0, [pT.r, vt.r], [o.r])
                    if kb == 0:
                        ob_ = obf.next()
                        k.evac(ob_.t[:, :, :], o.t[:, 0:256], [o.r], [ob_.r])
                        k.dma("sp", Osc[1].rearrange("(n p) f -> p n f", p=128)[:, 4 * qg:4 * qg + 4, hd * 64:(hd + 1) * 64],
                              ob_.t[:, :, :], [ob_.r], [dres])

                pipeline(items, [s0, s1, s2, s3, s4])
                P.flush()

        def phase_F():
            with ExitStack() as s:
                g = k.sb(s, "gfin", [128, D], F32)
                yb = k.sb(s, "yb", [128, 4, D], F32)
                k.dma("sp", g.t[:, :], I["gains"][8], [], [g.r])
                for t in range(4):
                    k.rmsnorm(lambda j: h.t[:, t * 4 + j, :], lambda j: list(hr[t * 4 + j]), 4, g, lambda j: yb.t[:, j, :], yb.r, junk, ss, rstd)
                    k.dma("sp", y_out[t * 512:(t + 1) * 512, :].rearrange("(n p) f -> p n f", p=128), yb.t[:, :, :], [yb.r], [dres])
                P.flush()

        def dump_h():
            with ExitStack() as s:
                k.dma("sp", y_out.rearrange("(n p) f -> p n f", p=128), h.t[:, :, :], hall, [dres])
                P.flush()

        def mixer(layer):
            w_o_in = I["w_out0"] if layer == 0 else I["w_sbo"]
            cmap = (lambda r, c4: (c4 if c4 < 2 else 2 + c4) + 2 * r) if layer == 0 else (lambda r, c4: r * 4 + c4)
            with ExitStack() as sx, ExitStack() as sr:
                wkv = k.sb(sr, "wkv", [128, 8, 2 * D], BF16, side="right")

                def pre_a():
                    for half in range(2):
                        k.dma("pool", wkv.t[:, :, half * D:(half + 1) * D],
                              I["xa_wkv"][layer][:, half * D:(half + 1) * D].rearrange("(k p) c -> p k c", p=128), [], [wkv.r])

                if layer == 0:
                    phase_A0(pre_a)
                else:
                    phase_A1(pre_a)
                wout = k.sb(sx, "wout", [128, 8, D], BF16)
                wq = k.sb(sx, "wq", [128, 8, D], BF16)
                wo = k.sb(sx, "wo", [128, 8, D], BF16)

                def pre_x():
                    k.dma("pool", wout.t[:, :, :], w_o_in.rearrange("(k p) c -> p k c", p=128), [], [wout.r])
                    k.dma("pool", wq.t[:, :, :], I["xa_wq"][layer].rearrange("(k p) c -> p k c", p=128), [], [wq.r])
                    k.dma("pool", wo.t[:, :, :], I["xa_wo"][layer].rearrange("(k p) c -> p k c", p=128), [], [wo.r])

                phase_X([(Osc[layer][q * OWN:(q + 1) * OWN, :], Og[layer][q]) for q in range(2)], pre_x, flush=False)
                phase_Ra(layer, Og[layer], cmap, wkv, sr.close, wout, wq, wo)

        stages = [
            ("P0", phase_P0),
            ("Ra0", lambda: mixer(0)),
            ("Rb0", lambda: phase_Rb(0)),
            ("E1", phase_E1),
            ("XH", lambda: phase_X([(Hn[q * 1024:(q + 1) * 1024, :], Hg[q]) for q in range(2)])),
            ("P1", phase_P1),
            ("Ra1", lambda: mixer(1)),
            ("Rb1", lambda: phase_Rb(1)),
        ]
        done = False
        for name, fn in stages:
            fn()
            if stop_after == name:
                dump_h()
                done = True
                break
        if not done:
            phase_F()
    return nc


def _consts(c):
    ident = np.eye(128, dtype=np.float32)
    sel0 = ident * (1.0 if c == 0 else 0.0)
    sel1 = ident * (1.0 if c == 1 else 0.0)
    jj = np.arange(128)[:, None]
    ii = np.arange(128)[None, :]
    trineg = np.where(jj >= ii, -1.0, 0.0).astype(np.float32)
    negones = -np.ones((128, 128), np.float32)
    sbmask = (jj < ii).astype(np.float32)
    ones = np.ones((128, 128), np.float32)
    cmat = np.stack([ident, sel0, sel1, trineg, negones, sbmask, ones]).astype(np.float32)
    half = 32
    inv = (10000.0 ** (-np.arange(half, dtype=np.float32) / half)).astype(np.float32)
    ang = (np.arange(S, dtype=np.float32)[:, None] * inv[None, :]).astype(np.float32)
    cos = np.cos(ang).astype(np.float32).T
    sin = np.sin(ang).astype(np.float32).T
    rope = np.stack([np.concatenate([cos, cos], 0), np.concatenate([-sin, sin], 0)]).astype(np.float32)
    slopes = 2.0 ** (-8.0 * np.arange(1, 5, dtype=np.float32) / 4)
    btab = np.zeros((2, 128, 32), np.float32)
    dtab = np.zeros((7, 128, 128), np.float32)
    chunkmask = (jj >= 64) & (ii < 64)
    jf = np.arange(128, dtype=np.float32)[:, None]
    gi = np.arange(32, dtype=np.float32)[None, :]
    iif = ii.astype(np.float32)
    adist = np.abs(ii - jj).astype(np.float32)
    sl0 = slopes[2 * c]
    btab[0] = sl0 * (jf - 128.0 * (gi - 1.0) - 64.0)
    for par in range(2):
        dtab[par] = np.where(chunkmask, NEG, -sl0 * adist + sl0 * iif + sl0 * (128.0 * par - 64.0))
    sl1 = slopes[2 * c + 1]
    btab[1] = sl1 * (jf - 128.0 * (gi - 3.0) - 192.0)
    for n in range(4):
        dtab[2 + n] = np.where(chunkmask, NEG, -sl1 * adist + sl1 * iif + sl1 * (128.0 * n - 192.0))
    dtab[6] = np.where(chunkmask, NEG, 0.0)
    return cmat, rope, btab, dtab


def _core_inputs(b, c, a):
    f = lambda v: np.ascontiguousarray(v, dtype=np.float32)
    rep = lambda v, n=128: np.ascontiguousarray(np.broadcast_to(np.asarray(v, np.float32).reshape(1, -1), (n, np.asarray(v).size)))
    x = a["x"][b]
    cmat, rope, btab, dtab = _consts(c)
    w_in = a["ev_w_in"][0]
    dh = [2 * c, 2 * c + 1]
    cols = []
    for hh in dh:
        cols.append(np.arange(hh * 128, (hh + 1) * 128))
    for hh in dh:
        cols.append(512 + np.arange(hh * 128, (hh + 1) * 128))
    for hh in dh:
        cols.append(1024 + np.arange(hh * 128, (hh + 1) * 128))
    cols.append(np.arange(1536, 1792))
    cols.append(np.arange(1792, 1920))
    cols.append(np.arange(1920, 1984))
    cols.append(np.concatenate([np.arange(1952, 1984), np.arange(1920, 1952)]))
    w_in0 = w_in[:, np.concatenate(cols)]
    w_uq = a["mla_w_uq"][0]
    ucols = []
    for hh in dh:
        ucols.append(np.arange(hh * 192, hh * 192 + 128))
    for hh in dh:
        base = hh * 192 + 128
        ucols.append(np.arange(base, base + 64))
        ucols.append(np.concatenate([np.arange(base + 32, base + 64), np.arange(base, base + 32)]))
    w_uq_o = w_uq[:, np.concatenate(ucols)]
    w_ukv = a["mla_w_ukv"][0]
    kcols = []
    for hh in dh:
        kcols.append(np.arange(hh * 256, hh * 256 + 128))
    for hh in dh:
        kcols.append(np.arange(hh * 256 + 128, hh * 256 + 256))
    w_ukv_o = w_ukv[:, np.concatenate(kcols)]
    sbw = a["sb_w_in"][0]
    sh = np.arange(c * 512, (c + 1) * 512)
    w_sb = sbw[:, np.concatenate([sh, 1024 + sh, 2048 + sh])]
    gains = np.stack([rep(a["ev_norm"][0]), rep(a["od_norm"][0]), rep(a["xa_norm"][0]), rep(a["xa_norm"][1]),
                      rep(a["xa_mem_norm"][0]), rep(a["xa_mem_norm"][1]), rep(a["mlp_norm"][0]), rep(a["mlp_norm"][1]),
                      rep(a["final_norm"])])
    lamv = np.stack([rep(a["diff_lq1"][0]), rep(a["diff_lk1"][0]), rep(a["diff_lq2"][0]), rep(a["diff_lk2"][0])], axis=1)
    return {
        "x_full": f(x), "x_own": f(x[c * OWN:(c + 1) * OWN]), "mem": f(a["mem"][b]), "gains": f(gains),
        "w_in0": f(w_in0), "w_uq": f(w_uq_o), "w_ukv": f(w_ukv_o), "w_out0": f(a["ev_w_out"][0]),
        "g_cq": f(np.asarray(a["mla_g_cq"][0]).reshape(2, 128).T), "g_ckv": f(np.asarray(a["mla_g_ckv"][0]).reshape(128, 1)),
        "g_sub": f(rep(a["diff_subln"][0])), "lamv": f(lamv),
        "w_sb": f(w_sb), "w_sbo": f(a["sb_w_out"][0]),
        "xa_wq": f(a["xa_wq"]), "xa_wkv": f(a["xa_wkv"]), "xa_wo": f(a["xa_wo"]), "w1": f(a["mlp_w1"]), "w2": f(a["mlp_w2"]),
        "cmat": f(cmat), "rope": f(rope), "btab": f(btab), "dtab": f(dtab),
    }


_NC_CACHE = {}


def run(inputs, stop_after=None):
    a = {k_: np.asarray(v) for k_, v in inputs.items()}
    if stop_after not in _NC_CACHE:
        _NC_CACHE[stop_after] = build_program(stop_after)
    nc = _NC_CACHE[stop_after]
    in_maps = [_core_inputs(core // 2, core % 2, a) for core in range(8)]
    res = run_bass_kernel_spmd(nc, in_maps, core_ids=list(range(8)))
    out = np.empty((4, S, D), np.float32)
    for core in range(8):
        b, c = core // 2, core % 2
        out[b, c * OWN:(c + 1) * OWN] = res.results[core]["y"]
    return out


def kernel(**inputs):
    return run(inputs)
```
